# Optimizing a Trainium2 kernel written in Bass

```python
import math
import jax
import jax.numpy as jnp
from jax import lax
import numpy as np

D_MODEL = 1024
BATCH = 4
SEQ = 4096
DEPTH = 4
DEC_BATCH = 32
DEC_SEQ = 1
PAST_LEN = 8192
PAGE_SIZE = 128

N_MIXERS = 2
N_LAYERS_A = (DEPTH + 1) // 2
N_LAYERS_B = DEPTH // 2
EPS = 1e-6
NEG_INF = -1e30

GDN_QK_HEADS = 8
GDN_V_HEADS = 16
GDN_DK = 128
GDN_DV = 128
GDN_CONV = 4
GDN_CHUNK = 64
GDN_QK_DIM = GDN_QK_HEADS * GDN_DK
GDN_V_DIM = GDN_V_HEADS * GDN_DV
GDN_CONV_DIM = 2 * GDN_QK_DIM + GDN_V_DIM
GDN_IN_DIM = GDN_CONV_DIM + GDN_V_DIM + 2 * GDN_V_HEADS

DSW_GROUPS = ((128, 1), (512, 4), (2048, 16))
DSW_N_GROUPS = len(DSW_GROUPS)
DSW_HEADS = 8
DSW_HD = 128
DSW_WIDTH = DSW_HEADS * DSW_HD
DSW_IN_DIM = DSW_N_GROUPS * 3 * DSW_WIDTH

D_FF = 4 * D_MODEL

kernel_name = 'gdn_dilated_swa_hybrid_step'


def rmsnorm(x, g):
    xf = x.astype(jnp.float32)
    y = xf * lax.rsqrt(jnp.mean(xf * xf, axis=-1, keepdims=True) + EPS)
    return (y * g.astype(jnp.float32)).astype(x.dtype)


def l2norm(x):
    xf = x.astype(jnp.float32)
    return xf * lax.rsqrt(jnp.sum(xf * xf, axis=-1, keepdims=True) + EPS)


def softmax_with_lse(s):
    m = jnp.max(s, axis=-1, keepdims=True)
    e = jnp.exp(s - m)
    den = jnp.sum(e, axis=-1, keepdims=True)
    return e / den, (m + jnp.log(den))[..., 0]


def sqrelu_mlp(x, w_up, w_down):
    return jnp.square(jax.nn.relu(x @ w_up)) @ w_down


def causal_conv_silu(x, buf, w):
    T = x.shape[1]
    xe = jnp.concatenate([buf.astype(x.dtype), x], axis=1)
    y = sum(xe[:, j:j + T] * w[j] for j in range(GDN_CONV))
    return jax.nn.silu(y), xe[:, T:]


def gated_delta_chunked(q, k, v, g, beta, s0):
    B, T, H, dk = k.shape
    dv = v.shape[-1]
    C = min(GDN_CHUNK, T)
    pad = (-T) % C
    nc = (T + pad) // C

    def prep(a):
        a = jnp.pad(a, [(0, 0), (0, pad)] + [(0, 0)] * (a.ndim - 2))
        a = a.reshape((B, nc, C) + a.shape[2:])
        return jnp.moveaxis(a, 3, 1)

    q, k, v, g, beta = prep(q), prep(k), prep(v), prep(g), prep(beta)
    gc = jnp.cumsum(g, axis=-1)
    causal = jnp.tril(jnp.ones((C, C), dtype=bool))
    strict = jnp.tril(jnp.ones((C, C), dtype=bool), -1)
    diff = gc[..., :, None] - gc[..., None, :]
    decay = jnp.where(causal, jnp.exp(jnp.where(causal, diff, 0.0)), 0.0)
    kb = k * beta[..., None]
    a_strict = jnp.where(strict, jnp.einsum('bhnid,bhnjd->bhnij', kb, k) * decay, 0.0)
    eye = jnp.eye(C, dtype=jnp.float32)
    t_inv = lax.linalg.triangular_solve(eye + a_strict, jnp.broadcast_to(eye, a_strict.shape),
                                        left_side=True, lower=True, unit_diagonal=True)
    u = t_inv @ (v * beta[..., None])
    w = t_inv @ (kb * jnp.exp(gc)[..., None])
    attn = jnp.einsum('bhnid,bhnjd->bhnij', q, k) * decay
    qg = q * jnp.exp(gc)[..., None]
    kd = k * jnp.exp(gc[..., -1:] - gc)[..., None]
    glast = jnp.exp(gc[..., -1])
    xs = (jnp.moveaxis(u, 2, 0), jnp.moveaxis(w, 2, 0), jnp.moveaxis(attn, 2, 0),
          jnp.moveaxis(qg, 2, 0), jnp.moveaxis(kd, 2, 0), jnp.moveaxis(glast, 2, 0))

    def step(s, inp):
        u_c, w_c, a_c, qg_c, kd_c, gl_c = inp
        v_new = u_c - jnp.einsum('bhid,bhde->bhie', w_c, s)
        o = jnp.einsum('bhid,bhde->bhie', qg_c, s) + jnp.einsum('bhij,bhje->bhie', a_c, v_new)
        s = s * gl_c[..., None, None] + jnp.einsum('bhid,bhie->bhde', kd_c, v_new)
        return s, o

    s_final, o = lax.scan(step, s0, xs)
    o = jnp.moveaxis(jnp.moveaxis(o, 0, 2), 1, 3).reshape(B, T + pad, H, dv)[:, :T]
    return o, s_final


def gdn_mixer(x, conv_buf, s0, w_in, conv_w, a_log, dt_bias, o_norm, w_out):
    B, T, _ = x.shape
    f32 = jnp.float32
    proj = x @ w_in
    o1 = GDN_CONV_DIM
    o2 = o1 + GDN_V_DIM
    o3 = o2 + GDN_V_HEADS
    qkv, z, b, a = proj[..., :o1], proj[..., o1:o2], proj[..., o2:o3], proj[..., o3:]
    qkv, new_buf = causal_conv_silu(qkv, conv_buf, conv_w)
    rep = GDN_V_HEADS // GDN_QK_HEADS
    q = qkv[..., :GDN_QK_DIM].reshape(B, T, GDN_QK_HEADS, GDN_DK)
    k = qkv[..., GDN_QK_DIM:2 * GDN_QK_DIM].reshape(B, T, GDN_QK_HEADS, GDN_DK)
    v = qkv[..., 2 * GDN_QK_DIM:].reshape(B, T, GDN_V_HEADS, GDN_DV).astype(f32)
    q = jnp.repeat(l2norm(q), rep, axis=2) * (GDN_DK ** -0.5)
    k = jnp.repeat(l2norm(k), rep, axis=2)
    beta = jax.nn.sigmoid(b.astype(f32))
    g = -jnp.exp(a_log.astype(f32)) * jax.nn.softplus(a.astype(f32) + dt_bias.astype(f32))
    o, s_new = gated_delta_chunked(q, k, v, g, beta, s0.astype(f32))
    z = z.reshape(B, T, GDN_V_HEADS, GDN_DV).astype(f32)
    o = rmsnorm(o, o_norm) * jax.nn.silu(z)
    y = o.reshape(B, T, GDN_V_DIM).astype(x.dtype) @ w_out
    return y, new_buf, s_new.astype(s0.dtype)


def dsw_project(x, w_in, q_norm, k_norm):
    B, T, _ = x.shape
    proj = (x @ w_in).reshape(B, T, DSW_N_GROUPS, 3, DSW_HEADS, DSW_HD)
    q = rmsnorm(proj[:, :, :, 0], q_norm[:, None, :])
    k = rmsnorm(proj[:, :, :, 1], k_norm[:, None, :])
    v = proj[:, :, :, 2]
    return q, k, v


def dilated_band_attn(q, k, v, window, dil):
    B, T, H, hd = q.shape
    blk = window // dil
    P = (-T) % (dil * blk)
    Tp = T + P
    nb = Tp // (dil * blk)

    def blocks(a):
        a = jnp.pad(a, [(0, 0), (P, 0), (0, 0), (0, 0)])
        return a.reshape(B, nb, blk, dil, H, hd)

    def with_prev(a):
        prev = jnp.concatenate([jnp.zeros_like(a[:, :1]), a[:, :-1]], axis=1)
        return jnp.concatenate([prev, a], axis=2)

    qb = blocks(q)
    kk = with_prev(blocks(k))
    vv = with_prev(blocks(v))
    s = jnp.einsum('bnqrhd,bnkrhd->bnrhqk', qb, kk, preferred_element_type=jnp.float32) * (hd ** -0.5)
    n = jnp.arange(nb)[:, None, None, None]
    r = jnp.arange(dil)[None, :, None, None]
    qi = jnp.arange(blk)[None, None, :, None]
    ki = jnp.arange(2 * blk)[None, None, None, :]
    dist = blk + qi - ki
    key_pos = ((n - 1) * blk + ki) * dil + r
    valid = (dist >= 0) & (dist <= blk) & (key_pos >= P)
    s = jnp.where(valid[:, :, None], s, NEG_INF)
    p, lse = softmax_with_lse(s)
    o = jnp.einsum('bnrhqk,bnkrhd->bnqrhd', p.astype(vv.dtype), vv)
    o = o.reshape(B, Tp, H, hd)[:, P:]
    lse = jnp.transpose(lse, (0, 1, 4, 2, 3)).reshape(B, Tp, H)[:, P:]
    return o, lse


def dilated_gather_attn(q, k_ext, v_ext, window, dil):
    B, T, H, hd = q.shape
    L = k_ext.shape[1]
    nk = window // dil
    qpos = (L - T) + jnp.arange(T)
    idx = qpos[:, None] - dil * jnp.arange(nk + 1)[None, :]
    valid = idx >= 0
    idx = jnp.maximum(idx, 0)
    kg = jnp.take(k_ext, idx, axis=1)
    vg = jnp.take(v_ext, idx, axis=1)
    s = jnp.einsum('bthd,btjhd->bthj', q, kg, preferred_element_type=jnp.float32) * (hd ** -0.5)
    s = jnp.where(valid[:, None, :], s, NEG_INF)
    p, lse = softmax_with_lse(s)
    o = jnp.einsum('bthj,btjhd->bthd', p.astype(vg.dtype), vg)
    return o, lse


def merge_groups(outs, lses, dtype):
    wts = jax.nn.softmax(jnp.stack(lses, axis=0), axis=0)
    o = jnp.einsum('gbth,gbthd->bthd', wts, jnp.stack(outs, axis=0).astype(jnp.float32))
    B, T = o.shape[:2]
    return o.reshape(B, T, DSW_WIDTH).astype(dtype)


def dsw_prompt(x, w_in, q_norm, k_norm, w_out):
    T = x.shape[1]
    q, k, v = dsw_project(x, w_in, q_norm, k_norm)
    outs, lses, bufs = [], [], []
    for gi, (win, dil) in enumerate(DSW_GROUPS):
        o, l = dilated_band_attn(q[:, :, gi], k[:, :, gi], v[:, :, gi], win, dil)
        outs.append(o)
        lses.append(l)
        keep = min(win, T)
        bufs.append(jnp.stack([k[:, T - keep:, gi], v[:, T - keep:, gi]], axis=2))
    return merge_groups(outs, lses, x.dtype) @ w_out, bufs


def dsw_sample(x, past_bufs, w_in, q_norm, k_norm, w_out):
    T = x.shape[1]
    q, k, v = dsw_project(x, w_in, q_norm, k_norm)
    outs, lses, bufs = [], [], []
    for gi, (win, dil) in enumerate(DSW_GROUPS):
        buf = past_bufs[gi].astype(k.dtype)
        k_ext = jnp.concatenate([buf[:, :, 0], k[:, :, gi]], axis=1)
        v_ext = jnp.concatenate([buf[:, :, 1], v[:, :, gi]], axis=1)
        o, l = dilated_gather_attn(q[:, :, gi], k_ext, v_ext, win, dil)
        outs.append(o)
        lses.append(l)
        keep = min(win, buf.shape[1] + T)
        new_rows = jnp.stack([k[:, :, gi], v[:, :, gi]], axis=2)
        bufs.append(jnp.concatenate([buf, new_rows], axis=1)[:, -keep:])
    return merge_groups(outs, lses, x.dtype) @ w_out, bufs


def setup_inputs(seed: int = 0) -> dict:
    key = jax.random.key(seed)
    ks = jax.random.split(key, 21)
    f32 = jnp.float32

    def nrm(k, shape, scale):
        return jax.random.normal(k, shape, f32) * scale

    def gain(k, shape):
        return 1.0 + 0.02 * jax.random.normal(k, shape, f32)

    lens = [min(w, PAST_LEN) for (w, _) in DSW_GROUPS]
    dt = jnp.exp(jax.random.uniform(ks[12], (N_LAYERS_A, GDN_V_HEADS), f32,
                                    math.log(1e-3), math.log(1e-1)))
    return {
        'x_prompt': nrm(ks[0], (BATCH, SEQ, D_MODEL), 1.0),
        'x_sample': nrm(ks[1], (DEC_BATCH, DEC_SEQ, D_MODEL), 1.0),
        'state_gdn': nrm(ks[2], (N_LAYERS_A, DEC_BATCH, GDN_V_HEADS, GDN_DK, GDN_DV), 0.1),
        'state_conv': nrm(ks[3], (N_LAYERS_A, DEC_BATCH, GDN_CONV - 1, GDN_CONV_DIM), 1.0),
        'cache_kv_w128': nrm(ks[4], (N_LAYERS_B, DEC_BATCH, lens[0], 2, DSW_HEADS, DSW_HD), 1.0),
        'cache_kv_w512': nrm(ks[5], (N_LAYERS_B, DEC_BATCH, lens[1], 2, DSW_HEADS, DSW_HD), 1.0),
        'cache_kv_w2048': nrm(ks[6], (N_LAYERS_B, DEC_BATCH, lens[2], 2, DSW_HEADS, DSW_HD), 1.0),
        'norm_mix': gain(ks[7], (DEPTH, D_MODEL)),
        'norm_mlp': gain(ks[8], (DEPTH, D_MODEL)),
        'gdn_w_in': nrm(ks[9], (N_LAYERS_A, D_MODEL, GDN_IN_DIM), D_MODEL ** -0.5),
        'gdn_conv_w': nrm(ks[10], (N_LAYERS_A, GDN_CONV, GDN_CONV_DIM), GDN_CONV ** -0.5),
        'gdn_a_log': jnp.log(jax.random.uniform(ks[11], (N_LAYERS_A, GDN_V_HEADS), f32, 1.0, 16.0)),
        'gdn_dt_bias': dt + jnp.log(-jnp.expm1(-dt)),
        'gdn_o_norm': gain(ks[13], (N_LAYERS_A, GDN_DV)),
        'gdn_w_out': nrm(ks[14], (N_LAYERS_A, GDN_V_DIM, D_MODEL), GDN_V_DIM ** -0.5),
        'dsw_w_in': nrm(ks[15], (N_LAYERS_B, D_MODEL, DSW_IN_DIM), D_MODEL ** -0.5),
        'dsw_q_norm': gain(ks[16], (N_LAYERS_B, DSW_N_GROUPS, DSW_HD)),
        'dsw_k_norm': gain(ks[17], (N_LAYERS_B, DSW_N_GROUPS, DSW_HD)),
        'dsw_w_out': nrm(ks[18], (N_LAYERS_B, DSW_WIDTH, D_MODEL), DSW_WIDTH ** -0.5),
        'mlp_w_up': nrm(ks[19], (DEPTH, D_MODEL, D_FF), D_MODEL ** -0.5),
        'mlp_w_down': nrm(ks[20], (DEPTH, D_FF, D_MODEL), D_FF ** -0.5),
    }


def reference(x_prompt, x_sample, state_gdn, state_conv, cache_kv_w128, cache_kv_w512, cache_kv_w2048,
              norm_mix, norm_mlp, gdn_w_in, gdn_conv_w, gdn_a_log, gdn_dt_bias, gdn_o_norm, gdn_w_out,
              dsw_w_in, dsw_q_norm, dsw_k_norm, dsw_w_out, mlp_w_up, mlp_w_down):
    B = x_prompt.shape[0]
    yp, ys = x_prompt, x_sample
    p_gdn, p_conv, s_gdn, s_conv = [], [], [], []
    p_kv = [[] for _ in DSW_GROUPS]
    s_kv = [[] for _ in DSW_GROUPS]
    for i in range(DEPTH):
        j = i // N_MIXERS
        hp = rmsnorm(yp, norm_mix[i])
        hs = rmsnorm(ys, norm_mix[i])
        if i % N_MIXERS == 0:
            wts = (gdn_w_in[j], gdn_conv_w[j], gdn_a_log[j], gdn_dt_bias[j], gdn_o_norm[j], gdn_w_out[j])
            conv0 = jnp.zeros((B, GDN_CONV - 1, GDN_CONV_DIM), hp.dtype)
            s0 = jnp.zeros((B, GDN_V_HEADS, GDN_DK, GDN_DV), state_gdn.dtype)
            op, cbp, sp = gdn_mixer(hp, conv0, s0, *wts)
            osm, cbs, ss = gdn_mixer(hs, state_conv[j], state_gdn[j], *wts)
            p_gdn.append(sp)
            p_conv.append(cbp)
            s_gdn.append(ss)
            s_conv.append(cbs)
        else:
            wts = (dsw_w_in[j], dsw_q_norm[j], dsw_k_norm[j], dsw_w_out[j])
            op, bufs_p = dsw_prompt(hp, *wts)
            osm, bufs_s = dsw_sample(hs, (cache_kv_w128[j], cache_kv_w512[j], cache_kv_w2048[j]), *wts)
            for gi in range(DSW_N_GROUPS):
                p_kv[gi].append(bufs_p[gi])
                s_kv[gi].append(bufs_s[gi])
        yp = yp + op
        ys = ys + osm
        yp = yp + sqrelu_mlp(rmsnorm(yp, norm_mlp[i]), mlp_w_up[i], mlp_w_down[i])
        ys = ys + sqrelu_mlp(rmsnorm(ys, norm_mlp[i]), mlp_w_up[i], mlp_w_down[i])
    return (yp, ys,
            jnp.stack(p_gdn), jnp.stack(p_conv),
            jnp.stack(p_kv[0]), jnp.stack(p_kv[1]), jnp.stack(p_kv[2]),
            jnp.stack(s_gdn), jnp.stack(s_conv),
            jnp.stack(s_kv[0]), jnp.stack(s_kv[1]), jnp.stack(s_kv[2]))
```

```python
import contextlib
import numpy as np
import concourse.bass as bass
import concourse.mybir as mybir
from concourse.bass_utils import run_bass_kernel_spmd

F32 = mybir.dt.float32
BF16 = mybir.dt.bfloat16
ALU = mybir.AluOpType
AF = mybir.ActivationFunctionType
AX = mybir.AxisListType

D = 1024
KC = 8
EPS = 1e-6
NS = 4
TT = 512


class Buf:
    __slots__ = ("w", "rs", "const", "psum")

    def __init__(self, const=False, psum=False):
        self.w = None
        self.rs = []
        self.const = const
        self.psum = psum


class Ins:
    __slots__ = ("eng", "fn", "deps", "needed", "idx", "dma", "sem", "target", "prev")

    def __init__(self, eng, fn, deps, dma=False):
        self.eng = eng
        self.fn = fn
        self.deps = deps
        self.needed = False
        self.idx = 0
        self.dma = dma
        self.sem = None
        self.target = 0
        self.prev = None


class V:
    def __init__(self, ap, bufs):
        self.ap = ap
        self.bufs = bufs

    def __getitem__(self, k):
        return V(self.ap[k], self.bufs)

    def rr(self, pat, **kw):
        return V(self.ap.rearrange(pat, **kw), self.bufs)

    def bc(self, shape):
        return V(self.ap.to_broadcast(list(shape)), self.bufs)

    def us(self, ax):
        return V(self.ap.unsqueeze(ax), self.bufs)

    def cast(self, dt):
        return V(self.ap.bitcast(dt), self.bufs)


class Prog:
    def __init__(self, nc, es):
        self.nc = nc
        self.es = es
        self.ins = []
        self.E = {"pe": nc.tensor, "act": nc.scalar, "dve": nc.vector, "pool": nc.gpsimd, "sp": nc.sync}
        self.engsem = {k: es.enter_context(nc.semaphore("es_" + k)) for k in ("pe", "act", "dve", "pool")}
        self.dsems = {q: [es.enter_context(nc.semaphore("ds_%s%d" % (q, i))) for i in range(n)]
                      for q, n in (("sp", 14), ("pool", 10), ("act", 4))}
        self.dcount = {q: 0 for q in self.dsems}
        self.dlast = {}
        self.dtot = {}

    def _deps(self, r, w):
        deps = {}
        for b in r:
            if b.psum:
                continue
            if b.w is not None:
                deps[id(b.w)] = (b.w, "RAW")
        for b in list(w) + [b for b in r if b.psum]:
            if b.w is not None and id(b.w) not in deps:
                deps[id(b.w)] = (b.w, "RAW" if b.psum else "WAW")
            for x in b.rs:
                if id(x) not in deps:
                    deps[id(x)] = (x, "WAR")
        return list(deps.values())

    def _commit(self, ins, r, w):
        for b in r:
            if b.psum:
                b.w = ins
                b.rs = []
            elif not b.const:
                b.rs.append(ins)
        for b in w:
            b.w = ins
            b.rs = []
        self.ins.append(ins)

    def op(self, eng, fn, r=(), w=()):
        ins = Ins(eng, fn, self._deps(r, w))
        self._commit(ins, r, w)
        return ins

    def dma(self, q, out, in_, own_sem=False, **kw):
        E = self.E[q]
        o, i = out.ap, in_.ap
        ins = Ins(q, (lambda: E.dma_start(out=o, in_=i, **kw)), self._deps(in_.bufs, out.bufs), dma=True)
        if own_sem:
            sem = self.es.enter_context(self.nc.semaphore("bulk%d" % len(self.dtot)))
        else:
            n = self.dcount[q]
            self.dcount[q] = n + 1
            sem = self.dsems[q][n % len(self.dsems[q])]
        ins.sem = sem
        ins.prev = self.dlast.get(id(sem))
        ins.target = (ins.prev.target if ins.prev else 0) + 16
        self.dlast[id(sem)] = ins
        self.dtot[id(sem)] = (q, sem, ins.target)
        self._commit(ins, in_.bufs, out.bufs)
        return ins

    def emit(self):
        def need(c, d, kind):
            if d.dma:
                return True
            if c.eng == d.eng:
                if c.dma:
                    return True
                if c.eng == "pe":
                    return False
                return kind == "RAW"
            return True

        cnt = {k: 0 for k in self.engsem}
        for c in self.ins:
            for d, kind in c.deps:
                if need(c, d, kind):
                    d.needed = True
        for c in self.ins:
            if (not c.dma) and c.needed:
                cnt[c.eng] += 1
                c.idx = cnt[c.eng]
        waited = {k: {} for k in self.E}
        for c in self.ins:
            E = self.E[c.eng]
            wl = {}
            for d, kind in c.deps:
                if not need(c, d, kind):
                    continue
                if d.dma:
                    s, v = d.sem, d.target
                else:
                    s, v = self.engsem[d.eng], d.idx
                if wl.get(id(s), (None, 0))[1] < v:
                    wl[id(s)] = (s, v)
            if c.dma and c.prev is not None:
                s, v = c.sem, c.prev.target
                if wl.get(id(s), (None, 0))[1] < v:
                    wl[id(s)] = (s, v)
            for s, v in wl.values():
                if waited[c.eng].get(id(s), 0) < v:
                    E.wait_ge(s, v)
                    waited[c.eng][id(s)] = v
            bi = c.fn()
            if c.dma:
                bi.then_inc(c.sem, 16)
            elif c.needed:
                bi.then_inc(self.engsem[c.eng], 1)
        for q, sem, tot in self.dtot.values():
            if waited[q].get(id(sem), 0) < tot:
                self.E[q].wait_ge(sem, tot)
        self.ins = []


def build(cfg):
    T = cfg["T"]
    DEPTH = cfg["DEPTH"]
    NLA = (DEPTH + 1) // 2
    NLB = DEPTH // 2
    NTP = T // TT
    NT = NTP + 1
    TX = T + TT
    nc = bass.Bass("TRN2", target_bir_lowering=False)
    es = contextlib.ExitStack()
    P = Prog(nc, es)
    E = P.E

    def dram(name, shape, dt, kind):
        return V(nc.dram_tensor(name, list(shape), dt, kind=kind).ap(), [Buf()])

    def dram_in(name, shape):
        v = dram(name, shape, F32, "ExternalInput")
        v.bufs = []
        return v

    x_prompt = dram_in("x_prompt", [T, D])
    x_sample = dram_in("x_sample", [NS, D])
    state_gdn = dram_in("state_gdn", [max(NLA, 1), NS, 16, 128, 128])
    state_conv = dram_in("state_conv", [max(NLA, 1), NS, 3, 4096])
    norm_mix = dram_in("norm_mix", [DEPTH, D])
    norm_mlp = dram_in("norm_mlp", [DEPTH, D])
    gdn_w_in = dram_in("gdn_w_in", [max(NLA, 1), D, 6176])
    gdn_conv_w = dram_in("gdn_conv_w", [max(NLA, 1), 4, 4096])
    gdn_a_log = dram_in("gdn_a_log", [max(NLA, 1), 16])
    gdn_dt_bias = dram_in("gdn_dt_bias", [max(NLA, 1), 16])
    gdn_o_norm = dram_in("gdn_o_norm", [max(NLA, 1), 128])
    gdn_w_out = dram_in("gdn_w_out", [max(NLA, 1), 2048, D])
    mlp_w_up = dram_in("mlp_w_up", [DEPTH, D, 4096])
    mlp_w_down = dram_in("mlp_w_down", [DEPTH, 4096, D])

    cmask = dram_in("cmask", [14, 128, 128])
    y_prompt = dram("y_prompt", [T, D], F32, "ExternalOutput")
    y_sample = dram("y_sample", [NS, D], F32, "ExternalOutput")
    p_gdn = dram("p_gdn", [max(NLA, 1), 16, 128, 128], F32, "ExternalOutput")
    p_conv = dram("p_conv", [max(NLA, 1), 3, 4096], F32, "ExternalOutput")
    s_gdn = dram("s_gdn", [max(NLA, 1), NS, 16, 128, 128], F32, "ExternalOutput")
    s_conv = dram("s_conv", [max(NLA, 1), NS, 3, 4096], F32, "ExternalOutput")

    GROUPS = ((128, 1), (512, 4), (2048, 16))
    if NLB:
        cache = [dram_in("cache_kv_w%d" % w_, [NLB, NS, w_, 2, 8, 128]) for w_, _ in GROUPS]
        dsw_w_in = dram_in("dsw_w_in", [NLB, D, 9216])
        dsw_q_norm = dram_in("dsw_q_norm", [NLB, 3, 128])
        dsw_k_norm = dram_in("dsw_k_norm", [NLB, 3, 128])
        dsw_w_out = dram_in("dsw_w_out", [NLB, 1024, D])
        p_kv = [dram("p_kv%d" % g_, [NLB, min(w_, T), 2, 8, 128], F32, "ExternalOutput")
                for g_, (w_, _) in enumerate(GROUPS)]
        s_kv = [dram("s_kv%d" % g_, [NLB, NS, w_, 2, 8, 128], F32, "ExternalOutput")
                for g_, (w_, _) in enumerate(GROUPS)]
        sw_din = dram("sw_din", [NLB, 18, 128, KC * 512], BF16, "Internal")
        sw_dout = dram("sw_dout", [NLB, 8, 128, 8 * 128], BF16, "Internal")
        QT_s = dram("QT_s", [3, 8, 128, TX], BF16, "Internal")
        KT_s = dram("KT_s", [3, 8, 128, TX], BF16, "Internal")
        V_s = dram("V_s", [3, TX, 8, 128], BF16, "Internal")
        OT_s = dram("OT_s", [8, 128, TX], BF16, "Internal")
        for jb_ in range(NLB):
            for g_, (w_, _) in enumerate(GROUPS):
                for s_ in range(NS):
                    P.dma("sp", V(s_kv[g_].ap[jb_, s_, 0:w_ - 1].rearrange("r a h d -> r (a h d)"), s_kv[g_].bufs),
                          V(cache[g_].ap[jb_, s_, 1:w_].rearrange("r a h d -> r (a h d)"), []), own_sem=True)

    sw_gin = dram("sw_gin", [max(NLA, 1), 12, 128, KC * 512], BF16, "Internal")
    sw_gba = dram("sw_gba", [max(NLA, 1), 128, KC * 32], BF16, "Internal")
    sw_gout = dram("sw_gout", [max(NLA, 1), 8, 128, 16 * 128], BF16, "Internal")
    sw_up = dram("sw_up", [DEPTH, 8, 128, KC * 512], BF16, "Internal")
    sw_dn = dram("sw_dn", [DEPTH, 8, 128, 32 * 128], BF16, "Internal")
    xs = dram("xs", [KC, 128, TX], F32, "Internal")

    def sb(name, shape, dt, const=False):
        t = es.enter_context(nc.sbuf_tensor(name, list(shape), dt))
        return V(t[:], [Buf(const=const)])

    banks = [V(es.enter_context(nc.psum_tensor("bank%d" % i, [128, 512], F32))[:], [Buf(psum=True)])
             for i in range(8)]
    bctr = [0]

    def bank():
        b = banks[bctr[0] % 8]
        bctr[0] += 1
        return b

    def mm(out, lhsT, rhs, start=True, stop=True):
        o, l, r = out.ap, lhsT.ap, rhs.ap
        P.op("pe", lambda: nc.tensor.matmul(o, lhsT=l, rhs=r, start=start, stop=stop),
             r=lhsT.bufs + rhs.bufs, w=out.bufs)

    def tr(out, in_, ident):
        o, i, d = out.ap, in_.ap, ident.ap
        P.op("pe", lambda: nc.tensor.transpose(o, i, d), r=in_.bufs + ident.bufs, w=out.bufs)

    def act(out, in_, func, bias=0.0, scale=1.0):
        o, i = out.ap, in_.ap
        P.op("act", lambda: nc.scalar.activation(out=o, in_=i, func=func, bias=bias, scale=scale),
             r=in_.bufs, w=out.bufs)

    def tt(eng, out, a, b, op):
        o, x, y = out.ap, a.ap, b.ap
        P.op(eng, lambda: E[eng].tensor_tensor(out=o, in0=x, in1=y, op=op), r=a.bufs + b.bufs, w=out.bufs)

    def ts(eng, out, a, s1, op0, s2=None, op1=None):
        o, x = out.ap, a.ap
        rb = list(a.bufs)
        if isinstance(s1, V):
            rb += s1.bufs
            s1 = s1.ap
        if isinstance(s2, V):
            rb += s2.bufs
            s2 = s2.ap
        if op1 is None:
            P.op(eng, lambda: E[eng].tensor_scalar(out=o, in0=x, scalar1=s1, scalar2=None, op0=op0), r=rb, w=out.bufs)
        else:
            P.op(eng, lambda: E[eng].tensor_scalar(out=o, in0=x, scalar1=s1, scalar2=s2, op0=op0, op1=op1),
                 r=rb, w=out.bufs)

    def stt(eng, out, a, s, b, op0, op1):
        o, x, y = out.ap, a.ap, b.ap
        rb = a.bufs + b.bufs
        if isinstance(s, V):
            rb = rb + s.bufs
            s = s.ap
        P.op(eng, lambda: E[eng].scalar_tensor_tensor(out=o, in0=x, scalar=s, in1=y, op0=op0, op1=op1),
             r=rb, w=out.bufs)

    def cp(eng, out, in_):
        o, i = out.ap, in_.ap
        if eng == "act":
            P.op("act", lambda: nc.scalar.copy(out=o, in_=i), r=in_.bufs, w=out.bufs)
        else:
            P.op(eng, lambda: E[eng].tensor_copy(out=o, in_=i), r=in_.bufs, w=out.bufs)

    def mset(eng, out, val):
        o = out.ap
        P.op(eng, lambda: E[eng].memset(o, val), r=[], w=out.bufs)

    def red(eng, out, in_, op=ALU.add):
        o, i = out.ap, in_.ap
        P.op(eng, lambda: E[eng].tensor_reduce(out=o, in_=i, axis=AX.X, op=op), r=in_.bufs, w=out.bufs)

    def recip(out, in_):
        o, i = out.ap, in_.ap
        P.op("dve", lambda: nc.vector.reciprocal(out=o, in_=i), r=in_.bufs, w=out.bufs)

    def asel(out, pattern, cmp, base, cm):
        o = out.ap
        P.op("pool", lambda: nc.gpsimd.affine_select(out=o, in_=o, pattern=pattern, compare_op=cmp, fill=0.0,
                                                     base=base, channel_multiplier=cm), r=out.bufs, w=out.bufs)

    ident_f = sb("ident_f", [128, 128], F32, True)
    ident_b = sb("ident_b", [128, 128], BF16, True)
    ones_b = sb("ones_b", [128, 128], BF16, True)
    Uincl_f = sb("Uincl_f", [128, 128], F32, True)
    Lstrict_f = sb("Lstrict_f", [128, 128], F32, True)
    lastsel_f = sb("lastsel_f", [128, 128], F32, True)
    Uincl4 = sb("Uincl4", [128, 4, 128], F32, True)
    Ustrict4 = sb("Ustrict4", [128, 4, 128], F32, True)
    ident4_b = sb("ident4_b", [128, 4, 128], BF16, True)
    SEL = sb("SEL", [16, 16, 128], F32, True)
    tokmask = sb("tokmask", [128, 1], F32, True)
    mask2 = sb("mask2", [128, 2, 128], F32, True)
    gtile = [sb("gtile%d" % i_, [128, 128], F32) for i_ in range(2)]

    mset("pool", ident_f, 1.0)
    asel(ident_f, [[-1, 128]], ALU.is_equal, 0, 1)
    cp("pool", ident_b, ident_f)
    mset("pool", ones_b, 1.0)
    mset("pool", Uincl_f, 1.0)
    asel(Uincl_f, [[1, 128]], ALU.is_ge, 0, -1)
    mset("pool", Lstrict_f, 1.0)
    asel(Lstrict_f, [[-1, 128]], ALU.is_gt, 0, 1)
    mset("pool", lastsel_f, 1.0)
    asel(lastsel_f, [[0, 128]], ALU.is_equal, -127, 1)
    mset("pool", Uincl4, 1.0)
    asel(Uincl4, [[0, 4], [1, 128]], ALU.is_ge, 0, -1)
    mset("pool", Ustrict4, 1.0)
    asel(Ustrict4, [[0, 4], [1, 128]], ALU.is_gt, 0, -1)
    for q_ in range(4):
        cp("pool", ident4_b[:, q_, :], ident_f)
    mset("pool", mask2, 1.0)
    asel(mask2[:, 0, :], [[-1, 128]], ALU.is_ge, 0, 1)
    asel(mask2[:, 1, :], [[1, 128]], ALU.is_ge, 0, -1)
    mset("pool", SEL, 1.0)
    asel(SEL, [[-1, 16], [0, 128]], ALU.is_equal, 0, 1)
    mset("pool", tokmask, 1.0)
    asel(tokmask, [[0, 1]], ALU.is_equal, 0, 1)

    lvl = sb("lvl", [128, 14, 128], BF16, True)
    P.dma("pool", lvl, V(cmask.ap.rearrange("m p f -> p m f"), []))
    gmix = sb("gmix", [128, DEPTH, KC], F32, True)
    gmlp = sb("gmlp", [128, DEPTH, KC], F32, True)
    P.dma("sp", gmix, V(norm_mix.ap.rearrange("l (k p) -> p l k", p=128), []), allow_slow_non_contiguous=True)
    P.dma("sp", gmlp, V(norm_mlp.ap.rearrange("l (k p) -> p l k", p=128), []), allow_slow_non_contiguous=True)
    if NLA:
        convw = sb("convw", [128, NLA, 4, 32], F32, True)
        P.dma("sp", convw, V(gdn_conv_w.ap.rearrange("l j (c p) -> p l j c", p=128), []),
              allow_slow_non_contiguous=True)
        onorm = sb("onorm", [128, NLA], F32, True)
        P.dma("sp", onorm, V(gdn_o_norm.ap.rearrange("l p -> p l"), []), allow_slow_non_contiguous=True)
        nA = sb("nA", [128, NLA, 16], F32, True)
        dtb = sb("dtb", [128, NLA, 16], F32, True)
        for l in range(NLA):
            P.dma("sp", nA[:, l, :], V(gdn_a_log.ap[l:l + 1, :].partition_broadcast(128), []))
            P.dma("sp", dtb[:, l, :], V(gdn_dt_bias.ap[l:l + 1, :].partition_broadcast(128), []))
        act(nA, nA, AF.Exp)
        ts("dve", nA, nA, -1.0, ALU.mult)

    def cast_layer(i):
        j = i // 2
        if i % 2 == 0:
            w = gdn_w_in.ap[j].rearrange("(k p) c -> p k c", p=128)
            for cg in range(12):
                P.dma("pool", V(sw_gin.ap[j, cg].rearrange("p (k c) -> p k c", k=KC), sw_gin.bufs),
                      V(w[:, :, cg * 512:(cg + 1) * 512], []))
            P.dma("pool", V(sw_gba.ap[j].rearrange("p (k c) -> p k c", k=KC), sw_gba.bufs),
                  V(w[:, :, 6144:6176], []))
            wo = gdn_w_out.ap[j].rearrange("(k p) c -> p k c", p=128)
            for dc in range(8):
                P.dma("pool", V(sw_gout.ap[j, dc].rearrange("p (k c) -> p k c", k=16), sw_gout.bufs),
                      V(wo[:, :, dc * 128:(dc + 1) * 128], []))
        else:
            w = dsw_w_in.ap[j].rearrange("(k p) c -> p k c", p=128)
            for cg in range(18):
                P.dma("pool", V(sw_din.ap[j, cg].rearrange("p (k c) -> p k c", k=KC), sw_din.bufs),
                      V(w[:, :, cg * 512:(cg + 1) * 512], []))
            wo = dsw_w_out.ap[j].rearrange("(k p) c -> p k c", p=128)
            for dc in range(8):
                P.dma("pool", V(sw_dout.ap[j, dc].rearrange("p (k c) -> p k c", k=8), sw_dout.bufs),
                      V(wo[:, :, dc * 128:(dc + 1) * 128], []))
        wu = mlp_w_up.ap[i].rearrange("(k p) c -> p k c", p=128)
        for cg in range(8):
            P.dma("pool", V(sw_up.ap[i, cg].rearrange("p (k c) -> p k c", k=KC), sw_up.bufs),
                  V(wu[:, :, cg * 512:(cg + 1) * 512], []))
        wd = mlp_w_down.ap[i].rearrange("(k p) c -> p k c", p=128)
        for dc in range(8):
            P.dma("pool", V(sw_dn.ap[i, dc].rearrange("p (k c) -> p k c", k=32), sw_dn.bufs),
                  V(wd[:, :, dc * 128:(dc + 1) * 128], []))

    for i in range(DEPTH):
        cast_layer(i)

    NW = 3
    wring = [sb("wring%d" % i, [128, 4096], BF16) for i in range(NW)]
    wctr = [0]

    def wload(src_ap, src_bufs, n):
        w = wring[wctr[0] % NW]
        wctr[0] += 1
        P.dma("sp", w[:, 0:n], V(src_ap, src_bufs))
        return w[:, 0:n]

    xT = sb("xT", [128, KC, TT], F32)
    hT = sb("hT", [128, KC, TT], BF16)
    sq = [sb("sq%d" % i, [128, TT], BF16) for i in range(2)]
    rstd = sb("rstd", [128, TT], F32)
    lnt = sb("lnt", [128, TT], F32)
    xtok = None
    pre = [sb("pre%d" % i, [128, TT + 3], F32) for i in range(2)]
    cyt = es.enter_context(nc.sbuf_tensor("cyt", [128, 2 * TT], F32))
    cy_b = [Buf(), Buf()]
    cy = [V(cyt[:, i_ * TT:(i_ + 1) * TT], [cy_b[i_]]) for i_ in range(2)]
    xtok = [V(cyt[:, :], cy_b)] * 2
    carry = sb("carry", [128, 32, 3], F32)
    arena = es.enter_context(nc.sbuf_tensor("arena", [128, 32 * TT], BF16))
    qk_buf, vt_buf = Buf(), Buf()
    qkT = V(arena[:, 0:16 * TT].rearrange("p (a t) -> p a t", a=16), [qk_buf])
    vT = [sb("vT%d" % i, [128, TT], BF16) for i in range(2)]
    v_tok = V(arena[:, 16 * TT:32 * TT].rearrange("p (s h d) -> p s h d", s=4, h=16), [vt_buf])
    k_tok = sb("k_tok", [128, 4, 8, 128], BF16)
    szT = sb("szT", [128, 16, TT], BF16)
    aT = V(arena[:, :].rearrange("p (a t) -> p a t", a=32), [qk_buf, vt_buf])
    rl = [sb("rl%d" % i, [128, TT], BF16) for i in range(2)]
    ba = sb("ba", [128, 4, 32], F32)
    beta = sb("beta", [128, 4, 16], F32)
    nbeta = sb("nbeta", [128, 4, 16], F32)
    gg = sb("gg", [128, 4, 16], F32)
    t16a = sb("t16a", [128, 4, 16], F32)
    t16b = sb("t16b", [128, 4, 16], F32)
    gc = sb("gc", [128, 16], F32)
    ngc = sb("ngc", [128, 16], F32)
    egc = sb("egc", [128, 16], F32)
    kds = sb("kds", [128, 16], F32)
    glast = sb("glast", [128, 16], F32)
    gcT = sb("gcT", [16, 128], F32)
    S = [sb("S%d" % i, [128, 4, 128], F32) for i in range(4)]
    S_bf = [sb("S_bf%d" % i, [128, 4, 128], BF16) for i in range(4)]
    o_tok = [sb("o_tok%d" % i, [128, 4, 128], F32) for i in range(2)] * 2
    on_tok = [sb("on_tok%d" % i, [128, 4, 128], BF16) for i in range(2)] * 2
    osq = [sb("osq0", [128, 4, 128], BF16)] * 2
    oss = sb("oss", [128, 16], F32)
    olnt = sb("olnt", [128, 16], F32)
    orstd = sb("orstd", [128, 16], F32)
    def hgbuf(name, dt):
        l = [sb("%s%d" % (name, i), [128, 4, 128], dt) for i in range(2)]
        return l + l
    g_DT = hgbuf("g_DT", F32)
    g_DTm = hgbuf("g_DTm", BF16)
    g_DTs = hgbuf("g_DTs", BF16)
    g_att = hgbuf("g_att", BF16)
    g_P = [hgbuf("g_Pa", BF16), hgbuf("g_Pb", BF16)]
    g_PT = [hgbuf("g_PTa", BF16), hgbuf("g_PTb", BF16)]
    g_R = [hgbuf("g_Ra", BF16), hgbuf("g_Rb", BF16)]
    g_kg = hgbuf("g_kg", BF16)
    g_kd = hgbuf("g_kd", BF16)
    g_nwT = hgbuf("g_nwT", BF16)
    g_vn = hgbuf("g_vn", BF16)
    g_t = hgbuf("g_t", F32)
    preS = sb("preS", [128, 32, NS, 4], F32)
    scv = sb("scv", [96, 128], F32)
    prodS = sb("prodS", [128, 32, NS, 4], F32)
    cyS = sb("cyS", [128, 32, NS], F32)
    cySb = sb("cySb", [128, 32, NS], F32)
    outS = sb("outS", [128, 128], F32)

    def pre_norm(gain, lidx, x_in=xT):
        bk = bank()
        for kc in range(KC):
            s = sq[kc % 2]
            act(s, x_in[:, kc, :], AF.Square)
            mm(bk, ones_b, s, start=(kc == 0), stop=(kc == KC - 1))
        act(lnt, bk, AF.Ln, bias=EPS, scale=1.0 / D)
        act(rstd, lnt, AF.Exp, scale=-0.5)
        for kc in range(KC):
            stt("dve", hT[:, kc, :], x_in[:, kc, :], gain[:, lidx, kc:kc + 1], rstd,
                ALU.mult, ALU.mult)

    def load_x_tile(ti, first):
        if not first:
            P.dma("sp", xT, V(xs.ap[:, :, ti * TT:(ti + 1) * TT].rearrange("k p t -> p k t"), xs.bufs))
            return
        for sub in range(4):
            xt = xtok[sub % 2]
            if ti < NTP:
                P.dma("sp", xt, V(x_prompt.ap[ti * TT + sub * 128: ti * TT + (sub + 1) * 128, :], []))
            else:
                mset("pool", xt, 0.0)
                P.dma("sp", xt[0:1, :], V(x_sample.ap[sub:sub + 1, :], []))
            for half in range(2):
                bk = bank()
                for q in range(4):
                    kc = half * 4 + q
                    tr(bk[:, q * 128:(q + 1) * 128], xt[:, kc * 128:(kc + 1) * 128], ident_f)
                cp("act" if half == 0 else "dve",
                   xT[:, half * 4:(half + 1) * 4, sub * 128:(sub + 1) * 128],
                   bk.rr("p (q t) -> p q t", q=4))

    def store_x_tile(ti, last):
        if not last:
            P.dma("pool", V(xs.ap[:, :, ti * TT:(ti + 1) * TT].rearrange("k p t -> p k t"), xs.bufs), xT)
            return
        for sub in range(4):
            xt = xtok[sub % 2]
            for half in range(2):
                bk = bank()
                for q in range(4):
                    kc = half * 4 + q
                    tr(bk[:, q * 128:(q + 1) * 128], xT[:, kc, sub * 128:(sub + 1) * 128], ident_f)
                cp("act" if half == 0 else "dve", xt[:, half * 512:(half + 1) * 512], bk)
            if ti < NTP:
                P.dma("pool", V(y_prompt.ap[ti * TT + sub * 128: ti * TT + (sub + 1) * 128, :], y_prompt.bufs), xt)
            else:
                P.dma("pool", V(y_sample.ap[sub:sub + 1, :], y_sample.bufs), xt[0:1, :])

    def mlp(i):
        pre_norm(gmlp, i)
        for cg in range(8):
            w = wload(sw_up.ap[i, cg], sw_up.bufs, KC * 512)
            w3 = w.rr("p (k c) -> p k c", k=KC)
            for q in range(4):
                bk = bank()
                for kc in range(KC):
                    mm(bk, w3[:, kc, q * 128:(q + 1) * 128], hT[:, kc, :], start=(kc == 0), stop=(kc == KC - 1))
                r = rl[q % 2]
                act(r, bk, AF.Relu)
                tt("pool" if q % 2 == 0 else "dve", aT[:, cg * 4 + q, :], r, r, ALU.mult)
        for dc in range(8):
            w = wload(sw_dn.ap[i, dc], sw_dn.bufs, 32 * 128)
            w3 = w.rr("p (k c) -> p k c", k=32)
            bk = bank()
            for kc in range(32):
                mm(bk, w3[:, kc, :], aT[:, kc, :], start=(kc == 0), stop=(kc == 31))
            tt("dve", xT[:, dc, :], xT[:, dc, :], bk, ALU.add)

    def gdn_layer(i, ti):
        j = i // 2
        sample = ti >= NTP
        pre_norm(gmix, i)
        if sample:
            for s in range(NS):
                P.dma("sp", scv, V(state_conv.ap[j, s].rearrange("j (c p) -> (j c) p", p=128), []))
                bk = bank()
                tr(bk[:, 0:96], scv, ident_f[0:96, 0:96])
                cp("act", preS[:, :, s, 0:3], bk[:, 0:96].rr("p (j c) -> p c j", j=3))
            mset("pool", qkT, 0.0)
        elif ti == 0:
            mset("pool", carry, 0.0)
            for hg in range(4):
                mset("pool", S[hg], 0.0)
                mset("pool", S_bf[hg], 0.0)

        def v_transpose(c, vsrc):
            bk2 = bank()
            bb = bk2.cast(BF16)
            for sub in range(4):
                tr(bb[:, sub * 128:(sub + 1) * 128], vsrc[:, sub * 128:(sub + 1) * 128], ident_b)
            cp("dve", v_tok[:, :, c - 16, :], bb[:, 0:512].rr("p (s d) -> p s d", s=4))

        for cg in range(12):
            w = wload(sw_gin.ap[j, cg], sw_gin.bufs, KC * 512)
            w3 = w.rr("p (k c) -> p k c", k=KC)
            for q in range(4):
                c = cg * 4 + q
                bk = bank()
                for kc in range(KC):
                    mm(bk, w3[:, kc, q * 128:(q + 1) * 128], hT[:, kc, :], start=(kc == 0), stop=(kc == KC - 1))
                if c >= 32:
                    act(szT[:, c - 32, :], bk, AF.Silu)
                    continue
                if sample:
                    cp("act", preS[:, c, :, 3], bk[:, 0:TT:128])
                    continue
                p_ = pre[c % 2]
                y = cy[c % 2]
                cp("pool", p_[:, 0:3], carry[:, c, :])
                cp("act", p_[:, 3:TT + 3], bk)
                cp("pool", carry[:, c, :], p_[:, TT:TT + 3])
                ts("pool", y, p_[:, 3:TT + 3], convw[:, j, 3, c:c + 1], ALU.mult)
                for tap in range(3):
                    stt("dve", y, p_[:, tap:tap + TT], convw[:, j, tap, c:c + 1], y, ALU.mult, ALU.add)
                if c < 16:
                    act(qkT[:, c, :], y, AF.Silu)
                else:
                    v_ = vT[c % 2]
                    act(v_, y, AF.Silu)
                    v_transpose(c, v_)
        if sample:
            for s in range(NS):
                tt("pool", prodS[:, :, s, :], preS[:, :, s, :], convw[:, j].rr("p j c -> p c j"), ALU.mult)
            red("dve", cyS, prodS)
            act(cySb, cyS, AF.Silu)
            cp("pool", qkT[:, :, 0:TT:128], cySb[:, 0:16, :])
            for c in range(16, 32):
                v_ = vT[c % 2]
                mset("pool", v_, 0.0)
                cp("pool", v_[:, 0:TT:128], cySb[:, c, :])
                v_transpose(c, v_)
            for s in range(NS):
                P.dma("pool", V(s_conv.ap[j, s, 0:2, :], s_conv.bufs), V(state_conv.ap[j, s, 1:3, :], []))
                bk = bank()
                tr(bk[0:32, 0:128], preS[:, :, s, 3], ident_f)
                cp("act", outS[0:32, :], bk[0:32, 0:128])
                P.dma("pool", V(s_conv.ap[j, s, 2, :].rearrange("(c p) -> c p", p=128), s_conv.bufs), outS[0:32, :])
        elif ti == NTP - 1:
            for jj in range(3):
                bk = bank()
                tr(bk[0:32, 0:128], carry[:, :, jj], ident_f)
                cp("act", outS[0:32, :], bk[0:32, 0:128])
                P.dma("pool", V(p_conv.ap[j, jj, :].rearrange("(c p) -> c p", p=128), p_conv.bufs), outS[0:32, :])

        for c in range(16):
            s = sq[c % 2]
            act(s, qkT[:, c, :], AF.Square)
            bk = bank()
            mm(bk, ones_b, s)
            act(lnt, bk, AF.Ln, bias=EPS)
            act(rstd, lnt, AF.Exp, scale=-0.5)
            stt("dve", qkT[:, c, :], qkT[:, c, :], (128.0 ** -0.5) if c < 8 else 1.0, rstd, ALU.mult, ALU.mult)
        for hk in range(8):
            bk2 = bank()
            bb = bk2.cast(BF16)
            for sub in range(4):
                tr(bb[:, sub * 128:(sub + 1) * 128], qkT[:, 8 + hk, sub * 128:(sub + 1) * 128], ident_b)
            cp("act", k_tok[:, :, hk, :], bb[:, 0:512].rr("p (s d) -> p s d", s=4))

        w = wload(sw_gba.ap[j], sw_gba.bufs, KC * 32)
        w3 = w.rr("p (k c) -> p k c", k=KC)
        bk = bank()
        for sub in range(4):
            for kc in range(KC):
                mm(bk[:, sub * 32:(sub + 1) * 32], hT[:, kc, sub * 128:(sub + 1) * 128], w3[:, kc, :],
                   start=(kc == 0), stop=(kc == KC - 1))
        cp("act", ba, bk[:, 0:128].rr("p (s c) -> p s c", s=4))
        act(t16a, ba[:, :, 0:16], AF.Exp, scale=-1.0)
        ts("dve", t16a, t16a, 1.0, ALU.add)
        recip(beta, t16a)
        tt("dve", t16b, ba[:, :, 16:32], dtb[:, j, :].us(1).bc([128, 4, 16]), ALU.add)
        ts("dve", t16a, t16b, -1.0, ALU.mult)
        tt("dve", t16a, t16a, t16b, ALU.max)
        act(t16a, t16a, AF.Exp, scale=-1.0)
        act(t16a, t16a, AF.Ln, bias=1.0)
        stt("dve", gg, t16b, 0.0, t16a, ALU.max, ALU.add)
        tt("dve", gg, gg, nA[:, j, :].us(1).bc([128, 4, 16]), ALU.mult)
        if sample:
            ts("dve", beta, beta, tokmask[:, 0:1], ALU.mult)
            ts("dve", gg, gg, tokmask[:, 0:1], ALU.mult)
        ts("dve", nbeta, beta, -1.0, ALU.mult)

        def b4(v):
            return v.us(2).bc([128, 4, 128])

        def r4(v):
            return v.rr("p (a b) i -> p a b i", a=2)

        for ch in range(4):
            cs = slice(ch * 128, (ch + 1) * 128)
            if sample:
                for hg in range(4):
                    P.dma("sp", S[hg], V(state_gdn.ap[j, ch, 4 * hg:4 * hg + 4].rearrange("h k v -> k h v"), []))
                    cp("pool", S_bf[hg], S[hg])
            g_c = gg[:, ch, :]
            bk = bank()
            mm(bk[:, 0:16], Uincl_f, g_c)
            mm(bk[:, 16:32], Lstrict_f, g_c)
            mm(bk[0:16, 32:160], g_c, Uincl_f)
            cp("act", gc, bk[:, 0:16])
            act(egc, bk[:, 0:16], AF.Exp)
            act(kds, bk[:, 16:32], AF.Exp)
            cp("act", gcT, bk[0:16, 32:160])
            ts("dve", ngc, gc, -1.0, ALU.mult)
            bk = bank()
            mm(bk[:, 0:16], lastsel_f, egc)
            cp("act", glast, bk[:, 0:16])
            for hp in range(2):
                for hg in (2 * hp, 2 * hp + 1):
                    hs = slice(4 * hg, 4 * hg + 4)
                    bkA = bank()
                    for q in range(4):
                        mm(bkA[:, q * 128:(q + 1) * 128], SEL[:, 4 * hg + q, :], gcT)
                    tt("dve", g_DT[hg], bkA.rr("p (q i) -> p q i", q=4), b4(ngc[:, hs]), ALU.add)
                    ts("dve", g_DT[hg], g_DT[hg], 0.0, ALU.min)
                    act(g_DT[hg], g_DT[hg], AF.Exp)
                    stt("dve", g_DTm[hg], g_DT[hg], 1.0, Uincl4, ALU.min, ALU.mult)
                    stt("dve", g_DTs[hg], g_DT[hg], 1.0, Ustrict4, ALU.min, ALU.mult)
                    tt("pool", g_DTs[hg], g_DTs[hg], b4(nbeta[:, ch, hs]), ALU.mult)
                    bkB = bank()
                    for a in range(2):
                        hk = 2 * hg + a
                        mm(bkB[:, a * 128:(a + 1) * 128], qkT[:, 8 + hk, cs], qkT[:, 8 + hk, cs])
                        mm(bkB[:, (2 + a) * 128:(3 + a) * 128], qkT[:, 8 + hk, cs], qkT[:, hk, cs])
                    kk = bkB[:, 0:256].rr("p (a i) -> p a i", a=2).us(2).bc([128, 2, 2, 128])
                    qk = bkB[:, 256:512].rr("p (a i) -> p a i", a=2).us(2).bc([128, 2, 2, 128])
                    tt("dve", r4(g_P[0][hg]), kk, r4(g_DTs[hg]), ALU.mult)
                    tt("dve", r4(g_att[hg]), qk, r4(g_DTm[hg]), ALU.mult)
                    bkC = bank().cast(BF16)
                    for q in range(4):
                        tr(bkC[:, q * 128:(q + 1) * 128], g_P[0][hg][:, q, :], ident_b)
                    cp("act", g_PT[0][hg], bkC[:, 0:512].rr("p (q i) -> p q i", q=4))
                    tt("pool", g_R[0][hg], g_P[0][hg], ident4_b, ALU.add)
                for hg in (2 * hp, 2 * hp + 1):
                    Tl, Ru, Xn, XnT = g_PT[1][hg], g_R[0][hg], g_P[1][hg], g_R[1][hg]
                    tt("pool", Xn, g_PT[0][hg], lvl[:, 0, :].us(1).bc([128, 4, 128]), ALU.mult)
                    tt("pool", XnT, g_P[0][hg], lvl[:, 1, :].us(1).bc([128, 4, 128]), ALU.mult)
                    tt("pool", Tl, Xn, ident4_b, ALU.add)
                    tt("pool", Ru, XnT, ident4_b, ALU.add)
                for k in range(2, 8):
                    for hg in (2 * hp, 2 * hp + 1):
                        Tl, Ru, Xn, XnT = g_PT[1][hg], g_R[0][hg], g_P[1][hg], g_R[1][hg]
                        Z1, Z2 = g_kg[hg], g_kd[hg]
                        last = (k == 7)
                        tt("pool", Xn, g_PT[0][hg], lvl[:, 2 * k - 2, :].us(1).bc([128, 4, 128]), ALU.mult)
                        if not last:
                            tt("pool", XnT, g_P[0][hg], lvl[:, 2 * k - 1, :].us(1).bc([128, 4, 128]), ALU.mult)
                            bk = bank()
                            for q in range(4):
                                mm(bk[:, q * 128:(q + 1) * 128], XnT[:, q, :], Tl[:, q, :])
                            cp("act", Z1, bk.rr("p (q i) -> p q i", q=4))
                        bk = bank()
                        for q in range(4):
                            mm(bk[:, q * 128:(q + 1) * 128], Xn[:, q, :], Ru[:, q, :])
                        cp("act" if last else "dve", Z2, bk.rr("p (q i) -> p q i", q=4))
                        if not last:
                            bkT_ = bank()
                            for q in range(4):
                                mm(bkT_[:, q * 128:(q + 1) * 128], Ru[:, q, :], Z1[:, q, :])
                        bkR_ = bank()
                        for q in range(4):
                            mm(bkR_[:, q * 128:(q + 1) * 128], Tl[:, q, :], Z2[:, q, :])
                        if not last:
                            tt("dve", Tl, Tl, bkT_.rr("p (q i) -> p q i", q=4), ALU.add)
                        tt("dve", Ru, Ru, bkR_.rr("p (q i) -> p q i", q=4), ALU.add)
                for hg in (2 * hp, 2 * hp + 1):
                    hs = slice(4 * hg, 4 * hg + 4)
                    R = g_R[0][hg]
                    ks = k_tok[:, ch, 2 * hg:2 * hg + 2, :].us(2).bc([128, 2, 2, 128])
                    tt("pool", r4(g_kg[hg]), ks, egc[:, hs].rr("p (a b) -> p a b", a=2).us(3).bc([128, 2, 2, 128]), ALU.mult)
                    tt("pool", r4(g_kd[hg]), ks, kds[:, hs].rr("p (a b) -> p a b", a=2).us(3).bc([128, 2, 2, 128]), ALU.mult)
                    bk = bank()
                    for q in range(4):
                        mm(bk[:, q * 128:(q + 1) * 128], g_kg[hg][:, q, :], R[:, q, :])
                    act(g_nwT[hg], bk.rr("p (q i) -> p q i", q=4), AF.Copy, scale=-1.0)
                    bk = bank()
                    for q in range(4):
                        mm(bk[:, q * 128:(q + 1) * 128], R[:, q, :], v_tok[:, ch, 4 * hg + q, :], start=True, stop=False)
                        mm(bk[:, q * 128:(q + 1) * 128], g_nwT[hg][:, q, :], S_bf[hg][:, q, :], start=False, stop=True)
                    tt("dve", g_vn[hg], bk.rr("p (q i) -> p q i", q=4), b4(beta[:, ch, hs]), ALU.mult)
                    bk1 = bank()
                    for q in range(4):
                        mm(bk1[:, q * 128:(q + 1) * 128], qkT[:, 2 * hg + q // 2, cs], S_bf[hg][:, q, :])
                    bk2 = bank()
                    for q in range(4):
                        mm(bk2[:, q * 128:(q + 1) * 128], g_att[hg][:, q, :], g_vn[hg][:, q, :])
                    cp("act", g_t[hg], bk2.rr("p (q i) -> p q i", q=4))
                    tt("dve", o_tok[hg], bk1.rr("p (q i) -> p q i", q=4), b4(egc[:, hs]), ALU.mult)
                    tt("pool", o_tok[hg], o_tok[hg], g_t[hg], ALU.add)
                    bk = bank()
                    for q in range(4):
                        mm(bk[:, q * 128:(q + 1) * 128], g_kd[hg][:, q, :], g_vn[hg][:, q, :])
                    tt("pool", S[hg], S[hg], b4(glast[:, hs]), ALU.mult)
                    tt("dve", S[hg], S[hg], bk.rr("p (q i) -> p q i", q=4), ALU.add)
                    cp("pool", S_bf[hg], S[hg])
                    act(osq[hg % 2], o_tok[hg], AF.Square)
                    red("dve", oss[:, hs], osq[hg % 2])

                    act(olnt[:, hs], oss[:, hs], AF.Ln, bias=EPS, scale=1.0 / 128)
                    act(orstd[:, hs], olnt[:, hs], AF.Exp, scale=-0.5)
                    tt("dve", on_tok[hg], o_tok[hg], b4(orstd[:, hs]), ALU.mult)
                    bkC = bank().cast(BF16)
                    for q in range(4):
                        tr(bkC[:, q * 128:(q + 1) * 128], on_tok[hg][:, q, :], ident_b)
                    stt("dve", szT[:, hs, cs], bkC[:, 0:512].rr("p (q t) -> p q t", q=4), onorm[:, j:j + 1],
                        szT[:, hs, cs], ALU.mult, ALU.mult)
            if sample:
                for hg in range(4):
                    P.dma("pool", V(s_gdn.ap[j, ch, 4 * hg:4 * hg + 4].rearrange("h k v -> k h v"), s_gdn.bufs), S[hg])
            elif ti == NTP - 1 and ch == 3:
                for hg in range(4):
                    P.dma("pool", V(p_gdn.ap[j, 4 * hg:4 * hg + 4].rearrange("h k v -> k h v"), p_gdn.bufs), S[hg])
        for dc in range(8):
            w = wload(sw_gout.ap[j, dc], sw_gout.bufs, 16 * 128)
            w3 = w.rr("p (k c) -> p k c", k=16)
            bk = bank()
            for kc in range(16):
                mm(bk, w3[:, kc, :], szT[:, kc, :], start=(kc == 0), stop=(kc == 15))
            tt("dve", xT[:, dc, :], xT[:, dc, :], bk, ALU.add)

    AB = [qk_buf, vt_buf]

    def out_kv(g, jb, kvi, half, src, t0, sub, sample):
        win, dil = GROUPS[g]
        hs = slice(half * 4, half * 4 + 4)
        if sample:
            P.dma("pool", V(s_kv[g].ap[jb, sub, win - 1:win, kvi, hs, :], s_kv[g].bufs), src[0:1])
        else:
            w0 = T - min(win, T)
            if t0 >= w0:
                P.dma("pool", V(p_kv[g].ap[jb, t0 - w0:t0 - w0 + 128, kvi, hs, :], p_kv[g].bufs), src)

    def dswA(i, ti):
        jb = i // 2
        sample = ti >= NTP
        pre_norm(gmix, i)
        tok0 = ti * TT
        for cgi in range(18):
            g, typ, half = cgi // 6, (cgi % 6) // 2, cgi % 2
            w = wload(sw_din.ap[jb, cgi], sw_din.bufs, KC * 512)
            w3 = w.rr("p (k c) -> p k c", k=KC)
            if typ < 2:
                gt = gtile[cgi % 2]
                src = (dsw_q_norm if typ == 0 else dsw_k_norm).ap[jb, g:g + 1, :].partition_broadcast(128)
                P.dma("sp", gt, V(src, []))
                tTa = V(arena[:, (cgi % 2) * 2048:(cgi % 2 + 1) * 2048].rearrange("p (h t) -> p h t", h=4), [qk_buf])
            for sub in range(4):
                bk = bank()
                for kc in range(KC):
                    mm(bk, hT[:, kc, sub * 128:(sub + 1) * 128], w3[:, kc, :], start=(kc == 0), stop=(kc == KC - 1))
                bk4 = bk.rr("p (h d) -> p h d", h=4)
                t0 = tok0 + sub * 128
                if typ == 2:
                    vf = g_DT[sub % 2]
                    vb = g_att[sub % 2]
                    cp("act", vf, bk4)
                    cp("pool", vb, vf)
                    P.dma("pool", V(V_s.ap[g, t0:t0 + 128, half * 4:half * 4 + 4, :], V_s.bufs), vb)
                    out_kv(g, jb, 1, half, vf, t0, sub, sample)
                else:
                    sqh = osq[0]
                    act(sqh, bk4, AF.Square)
                    red("dve", oss[:, 0:4], sqh)
                    act(olnt[:, 0:4], oss[:, 0:4], AF.Ln, bias=EPS, scale=1.0 / 128)
                    act(orstd[:, 0:4], olnt[:, 0:4], AF.Exp, scale=-0.5)
                    nf = o_tok[sub % 2]
                    nb = on_tok[sub % 2]
                    tt("dve", nf, bk4, orstd[:, 0:4].us(2).bc([128, 4, 128]), ALU.mult)
                    tt("pool", nf, nf, gt.us(1).bc([128, 4, 128]), ALU.mult)
                    if typ == 1:
                        out_kv(g, jb, 0, half, nf, t0, sub, sample)
                    cp("pool", nb, nf)
                    bkT = bank().cast(BF16)
                    for q in range(4):
                        tr(bkT[:, q * 128:(q + 1) * 128], nb[:, q, :], ident_b)
                    cp("act", tTa[:, :, sub * 128:(sub + 1) * 128], bkT[:, 0:512].rr("p (h t) -> p h t", h=4))
            if typ < 2:
                dst = QT_s if typ == 0 else KT_s
                P.dma("pool", V(dst.ap[g, half * 4:half * 4 + 4, :, tok0:tok0 + TT].rearrange("h p t -> p h t"),
                                dst.bufs), tTa)

    def dswB(i, ti):
        jb = i // 2
        P.dma("sp", szT[:, 0:8, :], V(OT_s.ap[:, :, ti * TT:(ti + 1) * TT].rearrange("h p t -> p h t"), OT_s.bufs))
        for dc in range(8):
            w = wload(sw_dout.ap[jb, dc], sw_dout.bufs, 8 * 128)
            w3 = w.rr("p (k c) -> p k c", k=8)
            bk = bank()
            for kc in range(8):
                mm(bk, w3[:, kc, :], szT[:, kc, :], start=(kc == 0), stop=(kc == 7))
            tt("dve", xT[:, dc, :], xT[:, dc, :], bk, ALU.add)

    def attn(i):
        jb = i // 2
        scale = 128.0 ** -0.5
        acc_o = V(szT.ap.rearrange("p a t -> p (a t)").bitcast(F32), szT.bufs)
        acc_d = V(xT.ap.rearrange("p a t -> p (a t)"), xT.bufs)
        qTg = V(arena[:, 0:TX], AB)
        kTg = V(arena[:, TX:2 * TX], AB)
        Vm = V(arena[:, 2 * TX:2 * TX + T], AB)
        Vs = V(arena[:, 2 * TX + T:3 * TX].rearrange("p (c e) -> p c e", c=4), AB)
        KVc = V(hT.ap.rearrange("p a t -> p (a t)")[:, 0:12 * 256].rearrange("p (x a e) -> p x a e", x=12, a=2), hT.bufs)
        kTc = V(wring[1].ap[:, 0:12 * 128].rearrange("p (x e) -> p x e", x=12), wring[1].bufs)
        obf = wring[0]
        obS = wring[2]
        accS_o = V(cy[0].ap[:, 0:512].rearrange("p (s q) -> p s q", s=4), cy[0].bufs)
        accS_d = V(cy[1].ap[:, 0:512].rearrange("p (s q) -> p s q", s=4), cy[1].bufs)
        Ef = [rstd, lnt]
        Eb = sq
        bctr_ = [0]

        def block(kprev, kcur, q, vprev, vcur, dst_o, dst_d, first):
            n = bctr_[0]
            bctr_[0] += 1
            ef = Ef[n % 2][:, 0:256]
            eb = Eb[n % 2][:, 0:256]
            bkS = bank()
            if kprev is not None:
                mm(bkS[:, 0:128], kprev, q)
            mm(bkS[:, 128:256], kcur, q)
            if kprev is not None:
                act(ef, bkS[:, 0:256], AF.Exp, scale=scale)
                tt("pool", eb.rr("p (a q) -> p a q", a=2), ef.rr("p (a q) -> p a q", a=2), mask2, ALU.mult)
            else:
                act(ef[:, 128:256], bkS[:, 128:256], AF.Exp, scale=scale)
                tt("pool", eb[:, 128:256], ef[:, 128:256], mask2[:, 1, :], ALU.mult)
            bkO = bank()
            if kprev is not None:
                mm(bkO[:, 0:128], vprev, eb[:, 0:128], start=True, stop=False)
                mm(bkO[:, 0:128], vcur, eb[:, 128:256], start=False, stop=True)
                mm(bkO[:, 128:256], ones_b, eb[:, 0:128], start=True, stop=False)
                mm(bkO[:, 128:256], ones_b, eb[:, 128:256], start=False, stop=True)
            else:
                mm(bkO[:, 0:128], vcur, eb[:, 128:256])
                mm(bkO[:, 128:256], ones_b, eb[:, 128:256])
            if first:
                cp("dve", dst_o, bkO[:, 0:128])
                cp("dve", dst_d, bkO[:, 128:256])
            else:
                tt("dve", dst_o, dst_o, bkO[:, 0:128], ALU.add)
                tt("dve", dst_d, dst_d, bkO[:, 128:256], ALU.add)

        for h in range(8):
            for s_ in range(NS):
                for g, (win, dil) in enumerate(GROUPS):
                    P.dma("pool", KVc[:, s_ * 3 + g], V(cache[g].ap[jb, s_, 0:win:dil, :, h, :], []))
            for x0 in range(0, 12, 4):
                bkT = bank().cast(BF16)
                for q in range(4):
                    tr(bkT[:, q * 128:(q + 1) * 128], KVc[:, x0 + q, 0, :], ident_b)
                cp("act", kTc[:, x0:x0 + 4, :], bkT[:, 0:512].rr("p (x e) -> p x e", x=4))
            for g, (win, dil) in enumerate(GROUPS):
                P.dma("sp", qTg, V(QT_s.ap[g, h], QT_s.bufs))
                P.dma("sp", kTg, V(KT_s.ap[g, h], KT_s.bufs))
                nbr = T // (128 * dil)
                Vm4 = Vm.rr("p (r n e) -> p r n e", r=dil, n=nbr)
                for r in range(dil):
                    P.dma("sp", Vm4[:, r], V(V_s.ap[g, 0:T, h, :].rearrange("(n i r) e -> r i n e", i=128, r=dil)[r],
                                           V_s.bufs))
                P.dma("sp", Vs, V(V_s.ap[g, T:TX, h, :].rearrange("(c i) e -> i c e", i=128), V_s.bufs))
                for r in range(dil):
                    for n in range(nbr):
                        lo = r + dil * 128 * n
                        qs = slice(lo, lo + dil * 127 + 1, dil)
                        if n > 0:
                            lp = lo - dil * 128
                            ps_ = slice(lp, lp + dil * 127 + 1, dil)
                            block(kTg[:, ps_], kTg[:, qs], qTg[:, qs], Vm4[:, r, n - 1, :], Vm4[:, r, n, :],
                                  acc_o[:, qs], acc_d[:, qs], g == 0)
                        else:
                            block(None, kTg[:, qs], qTg[:, qs], None, Vm4[:, r, n, :], acc_o[:, qs], acc_d[:, qs], g == 0)
                for s_ in range(NS):
                    qs = slice(T + s_ * 128, T + (s_ + 1) * 128)
                    block(kTc[:, s_ * 3 + g, :], kTg[:, qs], qTg[:, qs], KVc[:, s_ * 3 + g, 1, :], Vs[:, s_, :],
                          accS_o[:, s_, :], accS_d[:, s_, :], g == 0)
            recip(acc_d[:, 0:T], acc_d[:, 0:T])
            tt("dve", obf[:, 0:T], acc_o[:, 0:T], acc_d[:, 0:T], ALU.mult)
            P.dma("pool", V(OT_s.ap[h, :, 0:T], OT_s.bufs), obf[:, 0:T])
            recip(accS_d, accS_d)
            tt("dve", obS[:, 0:512].rr("p (s q) -> p s q", s=4), accS_o, accS_d, ALU.mult)
            P.dma("pool", V(OT_s.ap[h, :, T:TX], OT_s.bufs), obS[:, 0:512])

    for seg in range(NLB + 1):
        for ti in range(NT):
            load_x_tile(ti, seg == 0)
            if seg >= 1:
                dswB(2 * seg - 1, ti)
                mlp(2 * seg - 1)
            if 2 * seg < DEPTH:
                gdn_layer(2 * seg, ti)
                mlp(2 * seg)
            if 2 * seg + 1 < DEPTH:
                dswA(2 * seg + 1, ti)
            store_x_tile(ti, seg == NLB)
        if 2 * seg + 1 < DEPTH:
            attn(2 * seg + 1)
    P.emit()
    return nc


def _cmask():
    m = np.zeros((14, 128, 128), np.float32)
    i = np.arange(128)[:, None]
    j = np.arange(128)[None, :]
    for k in range(1, 8):
        b = 1 << k
        low = ((i // b) == (j // b)) & ((i % b) >= b // 2) & ((j % b) < b // 2)
        m[2 * (k - 1)] = low
        m[2 * (k - 1) + 1] = low.T
    return m


def run(cfg, in_maps):
    for m in in_maps:
        m["cmask"] = _cmask()
    nc = build(cfg)
    return run_bass_kernel_spmd(nc, in_maps, core_ids=list(range(len(in_maps))))


def kernel(x_prompt, x_sample, state_gdn, state_conv, cache_kv_w128, cache_kv_w512, cache_kv_w2048,
           norm_mix, norm_mlp, gdn_w_in, gdn_conv_w, gdn_a_log, gdn_dt_bias, gdn_o_norm, gdn_w_out,
           dsw_w_in, dsw_q_norm, dsw_k_norm, dsw_w_out, mlp_w_up, mlp_w_down):
    f = lambda a: np.ascontiguousarray(np.asarray(a), dtype=np.float32)
    B, T, _ = x_prompt.shape
    DB = x_sample.shape[0]
    DEPTH = norm_mix.shape[0]
    ncores = 8
    cfg = {"T": T, "DEPTH": DEPTH}
    shared = {"norm_mix": f(norm_mix), "norm_mlp": f(norm_mlp), "gdn_w_in": f(gdn_w_in),
              "gdn_conv_w": f(gdn_conv_w), "gdn_a_log": f(gdn_a_log), "gdn_dt_bias": f(gdn_dt_bias),
              "gdn_o_norm": f(gdn_o_norm), "gdn_w_out": f(gdn_w_out), "mlp_w_up": f(mlp_w_up),
              "mlp_w_down": f(mlp_w_down), "dsw_w_in": f(dsw_w_in), "dsw_q_norm": f(dsw_q_norm),
              "dsw_k_norm": f(dsw_k_norm), "dsw_w_out": f(dsw_w_out)}
    in_maps = []
    for c in range(ncores):
        b = c % B
        sl = slice(c * NS, (c + 1) * NS)
        m = dict(shared)
        m["x_prompt"] = f(x_prompt[b])
        m["x_sample"] = f(x_sample[sl, 0])
        m["state_gdn"] = f(state_gdn[:, sl])
        m["state_conv"] = f(state_conv[:, sl])
        m["cache_kv_w128"] = f(cache_kv_w128[:, sl])
        m["cache_kv_w512"] = f(cache_kv_w512[:, sl])
        m["cache_kv_w2048"] = f(cache_kv_w2048[:, sl])
        in_maps.append(m)
    res = run(cfg, in_maps).results
    NLA = (DEPTH + 1) // 2
    NLB = DEPTH // 2
    y_prompt = np.stack([res[b]["y_prompt"] for b in range(B)]).astype(np.float32)
    y_sample = np.concatenate([res[c]["y_sample"] for c in range(ncores)])[:, None, :].astype(np.float32)
    p_gdn = np.stack([res[b]["p_gdn"] for b in range(B)], axis=1).astype(np.float32)
    p_conv = np.stack([res[b]["p_conv"] for b in range(B)], axis=1).astype(np.float32)
    s_gdn = np.concatenate([res[c]["s_gdn"] for c in range(ncores)], axis=1).astype(np.float32)
    s_conv = np.concatenate([res[c]["s_conv"] for c in range(ncores)], axis=1).astype(np.float32)
    p_kv = [np.stack([res[b]["p_kv%d" % g] for b in range(B)], axis=1).astype(np.float32) for g in range(3)]
    s_kv = [np.concatenate([res[c]["s_kv%d" % g] for c in range(ncores)], axis=1).astype(np.float32) for g in range(3)]
    return (y_prompt, y_sample, p_gdn, p_conv, p_kv[0], p_kv[1], p_kv[2], s_gdn, s_conv, s_kv[0], s_kv[1], s_kv[2])
```

```python
import contextlib
import numpy as np
import concourse.bass as bass
import concourse.mybir as mybir
from concourse.bass_utils import run_bass_kernel_spmd

F32 = mybir.dt.float32
BF16 = mybir.dt.bfloat16
ALU = mybir.AluOpType
AF = mybir.ActivationFunctionType
AX = mybir.AxisListType

D = 1024
KC = 8
EPS = 1e-6
NS = 4
TT = 512


class Buf:
    __slots__ = ("w", "rs", "const", "psum")

    def __init__(self, const=False, psum=False):
        self.w = None
        self.rs = []
        self.const = const
        self.psum = psum


class Ins:
    __slots__ = ("eng", "fn", "deps", "needed", "idx", "dma", "sem", "target", "prev")

    def __init__(self, eng, fn, deps, dma=False):
        self.eng = eng
        self.fn = fn
        self.deps = deps
        self.needed = False
        self.idx = 0
        self.dma = dma
        self.sem = None
        self.target = 0
        self.prev = None


class V:
    def __init__(self, ap, bufs):
        self.ap = ap
        self.bufs = bufs

    def __getitem__(self, k):
        return V(self.ap[k], self.bufs)

    def rr(self, pat, **kw):
        return V(self.ap.rearrange(pat, **kw), self.bufs)

    def bc(self, shape):
        return V(self.ap.to_broadcast(list(shape)), self.bufs)

    def us(self, ax):
        return V(self.ap.unsqueeze(ax), self.bufs)

    def cast(self, dt):
        return V(self.ap.bitcast(dt), self.bufs)


class Prog:
    def __init__(self, nc, es):
        self.nc = nc
        self.es = es
        self.ins = []
        self.E = {"pe": nc.tensor, "act": nc.scalar, "dve": nc.vector, "pool": nc.gpsimd, "sp": nc.sync}
        self.engsem = {k: es.enter_context(nc.semaphore("es_" + k)) for k in ("pe", "act", "dve", "pool")}
        self.dsems = {q: [es.enter_context(nc.semaphore("ds_%s%d" % (q, i))) for i in range(n)]
                      for q, n in (("sp", 14), ("pool", 10), ("act", 4))}
        self.dcount = {q: 0 for q in self.dsems}
        self.dlast = {}
        self.dtot = {}

    def _deps(self, r, w):
        deps = {}
        for b in r:
            if b.psum:
                continue
            if b.w is not None:
                deps[id(b.w)] = (b.w, "RAW")
        for b in list(w) + [b for b in r if b.psum]:
            if b.w is not None and id(b.w) not in deps:
                deps[id(b.w)] = (b.w, "RAW" if b.psum else "WAW")
            for x in b.rs:
                if id(x) not in deps:
                    deps[id(x)] = (x, "WAR")
        return list(deps.values())

    def _commit(self, ins, r, w):
        for b in r:
            if b.psum:
                b.w = ins
                b.rs = []
            elif not b.const:
                b.rs.append(ins)
        for b in w:
            b.w = ins
            b.rs = []
        self.ins.append(ins)

    def op(self, eng, fn, r=(), w=()):
        ins = Ins(eng, fn, self._deps(r, w))
        self._commit(ins, r, w)
        return ins

    def dma(self, q, out, in_, own_sem=False, **kw):
        E = self.E[q]
        o, i = out.ap, in_.ap
        ins = Ins(q, (lambda: E.dma_start(out=o, in_=i, **kw)), self._deps(in_.bufs, out.bufs), dma=True)
        if own_sem:
            sem = self.es.enter_context(self.nc.semaphore("bulk%d" % len(self.dtot)))
        else:
            n = self.dcount[q]
            self.dcount[q] = n + 1
            sem = self.dsems[q][n % len(self.dsems[q])]
        ins.sem = sem
        ins.prev = self.dlast.get(id(sem))
        ins.target = (ins.prev.target if ins.prev else 0) + 16
        self.dlast[id(sem)] = ins
        self.dtot[id(sem)] = (q, sem, ins.target)
        self._commit(ins, in_.bufs, out.bufs)
        return ins

    def emit(self):
        def need(c, d, kind):
            if d.dma:
                return True
            if c.eng == d.eng:
                if c.dma:
                    return True
                if c.eng == "pe":
                    return False
                return kind == "RAW"
            return True

        cnt = {k: 0 for k in self.engsem}
        for c in self.ins:
            for d, kind in c.deps:
                if need(c, d, kind):
                    d.needed = True
        for c in self.ins:
            if (not c.dma) and c.needed:
                cnt[c.eng] += 1
                c.idx = cnt[c.eng]
        waited = {k: {} for k in self.E}
        for c in self.ins:
            E = self.E[c.eng]
            wl = {}
            for d, kind in c.deps:
                if not need(c, d, kind):
                    continue
                if d.dma:
                    s, v = d.sem, d.target
                else:
                    s, v = self.engsem[d.eng], d.idx
                if wl.get(id(s), (None, 0))[1] < v:
                    wl[id(s)] = (s, v)
            if c.dma and c.prev is not None:
                s, v = c.sem, c.prev.target
                if wl.get(id(s), (None, 0))[1] < v:
                    wl[id(s)] = (s, v)
            for s, v in wl.values():
                if waited[c.eng].get(id(s), 0) < v:
                    E.wait_ge(s, v)
                    waited[c.eng][id(s)] = v
            bi = c.fn()
            if c.dma:
                bi.then_inc(c.sem, 16)
            elif c.needed:
                bi.then_inc(self.engsem[c.eng], 1)
        for q, sem, tot in self.dtot.values():
            if waited[q].get(id(sem), 0) < tot:
                self.E[q].wait_ge(sem, tot)
        self.ins = []


def build(cfg):
    T = cfg["T"]
    DEPTH = cfg["DEPTH"]
    NLA = (DEPTH + 1) // 2
    NLB = DEPTH // 2
    NTP = T // TT
    NT = NTP + 1
    TX = T + TT
    nc = bass.Bass("TRN2", target_bir_lowering=False)
    es = contextlib.ExitStack()
    P = Prog(nc, es)
    E = P.E

    def dram(name, shape, dt, kind):
        return V(nc.dram_tensor(name, list(shape), dt, kind=kind).ap(), [Buf()])

    def dram_in(name, shape):
        v = dram(name, shape, F32, "ExternalInput")
        v.bufs = []
        return v

    x_prompt = dram_in("x_prompt", [T, D])
    x_sample = dram_in("x_sample", [NS, D])
    state_gdn = dram_in("state_gdn", [max(NLA, 1), NS, 16, 128, 128])
    state_conv = dram_in("state_conv", [max(NLA, 1), NS, 3, 4096])
    norm_mix = dram_in("norm_mix", [DEPTH, D])
    norm_mlp = dram_in("norm_mlp", [DEPTH, D])
    gdn_w_in = dram_in("gdn_w_in", [max(NLA, 1), D, 6176])
    gdn_conv_w = dram_in("gdn_conv_w", [max(NLA, 1), 4, 4096])
    gdn_a_log = dram_in("gdn_a_log", [max(NLA, 1), 16])
    gdn_dt_bias = dram_in("gdn_dt_bias", [max(NLA, 1), 16])
    gdn_o_norm = dram_in("gdn_o_norm", [max(NLA, 1), 128])
    gdn_w_out = dram_in("gdn_w_out", [max(NLA, 1), 2048, D])
    mlp_w_up = dram_in("mlp_w_up", [DEPTH, D, 4096])
    mlp_w_down = dram_in("mlp_w_down", [DEPTH, 4096, D])

    cmask = dram_in("cmask", [14, 128, 128])
    y_prompt = dram("y_prompt", [T, D], F32, "ExternalOutput")
    y_sample = dram("y_sample", [NS, D], F32, "ExternalOutput")
    p_gdn = dram("p_gdn", [max(NLA, 1), 16, 128, 128], F32, "ExternalOutput")
    p_conv = dram("p_conv", [max(NLA, 1), 3, 4096], F32, "ExternalOutput")
    s_gdn = dram("s_gdn", [max(NLA, 1), NS, 16, 128, 128], F32, "ExternalOutput")
    s_conv = dram("s_conv", [max(NLA, 1), NS, 3, 4096], F32, "ExternalOutput")

    GROUPS = ((128, 1), (512, 4), (2048, 16))
    if NLB:
        cache = [dram_in("cache_kv_w%d" % w_, [NLB, NS, w_, 2, 8, 128]) for w_, _ in GROUPS]
        dsw_w_in = dram_in("dsw_w_in", [NLB, D, 9216])
        dsw_q_norm = dram_in("dsw_q_norm", [NLB, 3, 128])
        dsw_k_norm = dram_in("dsw_k_norm", [NLB, 3, 128])
        dsw_w_out = dram_in("dsw_w_out", [NLB, 1024, D])
        p_kv = [dram("p_kv%d" % g_, [NLB, min(w_, T), 2, 8, 128], F32, "ExternalOutput")
                for g_, (w_, _) in enumerate(GROUPS)]
        s_kv = [dram("s_kv%d" % g_, [NLB, NS, w_, 2, 8, 128], F32, "ExternalOutput")
                for g_, (w_, _) in enumerate(GROUPS)]
        sw_din = dram("sw_din", [NLB, 18, 128, KC * 512], BF16, "Internal")
        sw_dout = dram("sw_dout", [NLB, 8, 128, 8 * 128], BF16, "Internal")
        QT_s = dram("QT_s", [3, 8, 128, TX], BF16, "Internal")
        KT_s = dram("KT_s", [3, 8, 128, TX], BF16, "Internal")
        V_s = dram("V_s", [3, TX, 8, 128], BF16, "Internal")
        OT_s = dram("OT_s", [8, 128, TX], BF16, "Internal")

    def bulk_copies(jb_):
        for g_, (w_, _) in enumerate(GROUPS):
            for s_ in range(NS):
                P.dma("sp", V(s_kv[g_].ap[jb_, s_, 0:w_ - 1].rearrange("r a h d -> r (a h d)"), s_kv[g_].bufs),
                      V(cache[g_].ap[jb_, s_, 1:w_].rearrange("r a h d -> r (a h d)"), []), own_sem=True)

    sw_gin = dram("sw_gin", [max(NLA, 1), 12, 128, KC * 512], BF16, "Internal")
    sw_gba = dram("sw_gba", [max(NLA, 1), 128, KC * 32], BF16, "Internal")
    sw_gout = dram("sw_gout", [max(NLA, 1), 8, 128, 16 * 128], BF16, "Internal")
    sw_up = dram("sw_up", [DEPTH, 8, 128, KC * 512], BF16, "Internal")
    sw_dn = dram("sw_dn", [DEPTH, 8, 128, 32 * 128], BF16, "Internal")
    xs = dram("xs", [KC, 128, TX], F32, "Internal")

    def sb(name, shape, dt, const=False):
        t = es.enter_context(nc.sbuf_tensor(name, list(shape), dt))
        return V(t[:], [Buf(const=const)])

    banks = [V(es.enter_context(nc.psum_tensor("bank%d" % i, [128, 512], F32))[:], [Buf(psum=True)])
             for i in range(8)]
    bctr = [0]

    def bank():
        b = banks[bctr[0] % 8]
        bctr[0] += 1
        return b

    def mm(out, lhsT, rhs, start=True, stop=True):
        o, l, r = out.ap, lhsT.ap, rhs.ap
        P.op("pe", lambda: nc.tensor.matmul(o, lhsT=l, rhs=r, start=start, stop=stop),
             r=lhsT.bufs + rhs.bufs, w=out.bufs)

    def tr(out, in_, ident):
        o, i, d = out.ap, in_.ap, ident.ap
        P.op("pe", lambda: nc.tensor.transpose(o, i, d), r=in_.bufs + ident.bufs, w=out.bufs)

    def act(out, in_, func, bias=0.0, scale=1.0):
        o, i = out.ap, in_.ap
        P.op("act", lambda: nc.scalar.activation(out=o, in_=i, func=func, bias=bias, scale=scale),
             r=in_.bufs, w=out.bufs)

    def tt(eng, out, a, b, op):
        o, x, y = out.ap, a.ap, b.ap
        P.op(eng, lambda: E[eng].tensor_tensor(out=o, in0=x, in1=y, op=op), r=a.bufs + b.bufs, w=out.bufs)

    def ts(eng, out, a, s1, op0, s2=None, op1=None):
        o, x = out.ap, a.ap
        rb = list(a.bufs)
        if isinstance(s1, V):
            rb += s1.bufs
            s1 = s1.ap
        if isinstance(s2, V):
            rb += s2.bufs
            s2 = s2.ap
        if op1 is None:
            P.op(eng, lambda: E[eng].tensor_scalar(out=o, in0=x, scalar1=s1, scalar2=None, op0=op0), r=rb, w=out.bufs)
        else:
            P.op(eng, lambda: E[eng].tensor_scalar(out=o, in0=x, scalar1=s1, scalar2=s2, op0=op0, op1=op1),
                 r=rb, w=out.bufs)

    def stt(eng, out, a, s, b, op0, op1):
        o, x, y = out.ap, a.ap, b.ap
        rb = a.bufs + b.bufs
        if isinstance(s, V):
            rb = rb + s.bufs
            s = s.ap
        P.op(eng, lambda: E[eng].scalar_tensor_tensor(out=o, in0=x, scalar=s, in1=y, op0=op0, op1=op1),
             r=rb, w=out.bufs)

    def cp(eng, out, in_):
        o, i = out.ap, in_.ap
        if eng == "act":
            P.op("act", lambda: nc.scalar.copy(out=o, in_=i), r=in_.bufs, w=out.bufs)
        else:
            P.op(eng, lambda: E[eng].tensor_copy(out=o, in_=i), r=in_.bufs, w=out.bufs)

    def mset(eng, out, val):
        o = out.ap
        P.op(eng, lambda: E[eng].memset(o, val), r=[], w=out.bufs)

    def red(eng, out, in_, op=ALU.add):
        o, i = out.ap, in_.ap
        P.op(eng, lambda: E[eng].tensor_reduce(out=o, in_=i, axis=AX.X, op=op), r=in_.bufs, w=out.bufs)

    def recip(out, in_):
        o, i = out.ap, in_.ap
        P.op("dve", lambda: nc.vector.reciprocal(out=o, in_=i), r=in_.bufs, w=out.bufs)

    def asel(out, pattern, cmp, base, cm):
        o = out.ap
        P.op("pool", lambda: nc.gpsimd.affine_select(out=o, in_=o, pattern=pattern, compare_op=cmp, fill=0.0,
                                                     base=base, channel_multiplier=cm), r=out.bufs, w=out.bufs)

    ident_f = sb("ident_f", [128, 128], F32, True)
    ident_b = sb("ident_b", [128, 128], BF16, True)
    ones_b = sb("ones_b", [128, 128], BF16, True)
    Uincl_f = sb("Uincl_f", [128, 128], F32, True)
    Lstrict_f = sb("Lstrict_f", [128, 128], F32, True)
    lastsel_f = sb("lastsel_f", [128, 128], F32, True)
    Uincl4 = sb("Uincl4", [128, 4, 128], F32, True)
    Ustrict4 = sb("Ustrict4", [128, 4, 128], F32, True)
    ident4_b = sb("ident4_b", [128, 4, 128], BF16, True)
    SEL = sb("SEL", [16, 16, 128], F32, True)
    tokmask = sb("tokmask", [128, 1], F32, True)
    mask2 = sb("mask2", [128, 2, 128], F32, True)
    gtile = [sb("gtile%d" % i_, [128, 128], F32) for i_ in range(2)]

    mset("pool", ident_f, 1.0)
    asel(ident_f, [[-1, 128]], ALU.is_equal, 0, 1)
    cp("pool", ident_b, ident_f)
    mset("pool", ones_b, 1.0)
    mset("pool", Uincl_f, 1.0)
    asel(Uincl_f, [[1, 128]], ALU.is_ge, 0, -1)
    mset("pool", Lstrict_f, 1.0)
    asel(Lstrict_f, [[-1, 128]], ALU.is_gt, 0, 1)
    mset("pool", lastsel_f, 1.0)
    asel(lastsel_f, [[0, 128]], ALU.is_equal, -127, 1)
    mset("pool", Uincl4, 1.0)
    asel(Uincl4, [[0, 4], [1, 128]], ALU.is_ge, 0, -1)
    mset("pool", Ustrict4, 1.0)
    asel(Ustrict4, [[0, 4], [1, 128]], ALU.is_gt, 0, -1)
    for q_ in range(4):
        cp("pool", ident4_b[:, q_, :], ident_f)
    mset("pool", mask2, 1.0)
    asel(mask2[:, 0, :], [[-1, 128]], ALU.is_ge, 0, 1)
    asel(mask2[:, 1, :], [[1, 128]], ALU.is_ge, 0, -1)
    mset("pool", SEL, 1.0)
    asel(SEL, [[-1, 16], [0, 128]], ALU.is_equal, 0, 1)
    mset("pool", tokmask, 1.0)
    asel(tokmask, [[0, 1]], ALU.is_equal, 0, 1)

    lvl = sb("lvl", [128, 14, 128], BF16, True)
    P.dma("pool", lvl, V(cmask.ap.rearrange("m p f -> p m f"), []))
    gmix = sb("gmix", [128, DEPTH, KC], F32, True)
    gmlp = sb("gmlp", [128, DEPTH, KC], F32, True)
    P.dma("sp", gmix, V(norm_mix.ap.rearrange("l (k p) -> p l k", p=128), []), allow_slow_non_contiguous=True)
    P.dma("sp", gmlp, V(norm_mlp.ap.rearrange("l (k p) -> p l k", p=128), []), allow_slow_non_contiguous=True)
    if NLA:
        convw = sb("convw", [128, NLA, 4, 32], F32, True)
        P.dma("sp", convw, V(gdn_conv_w.ap.rearrange("l j (c p) -> p l j c", p=128), []),
              allow_slow_non_contiguous=True)
        onorm = sb("onorm", [128, NLA], F32, True)
        P.dma("sp", onorm, V(gdn_o_norm.ap.rearrange("l p -> p l"), []), allow_slow_non_contiguous=True)
        nA = sb("nA", [128, NLA, 16], F32, True)
        dtb = sb("dtb", [128, NLA, 16], F32, True)
        for l in range(NLA):
            P.dma("sp", nA[:, l, :], V(gdn_a_log.ap[l:l + 1, :].partition_broadcast(128), []))
            P.dma("sp", dtb[:, l, :], V(gdn_dt_bias.ap[l:l + 1, :].partition_broadcast(128), []))
        act(nA, nA, AF.Exp)
        ts("dve", nA, nA, -1.0, ALU.mult)

    def cast_layer(i):
        j = i // 2
        if i % 2 == 0:
            w = gdn_w_in.ap[j].rearrange("(k p) c -> p k c", p=128)
            for cg in range(12):
                P.dma("pool", V(sw_gin.ap[j, cg].rearrange("p (k c) -> p k c", k=KC), sw_gin.bufs),
                      V(w[:, :, cg * 512:(cg + 1) * 512], []))
            P.dma("pool", V(sw_gba.ap[j].rearrange("p (k c) -> p k c", k=KC), sw_gba.bufs),
                  V(w[:, :, 6144:6176], []))
            wo = gdn_w_out.ap[j].rearrange("(k p) c -> p k c", p=128)
            for dc in range(8):
                P.dma("pool", V(sw_gout.ap[j, dc].rearrange("p (k c) -> p k c", k=16), sw_gout.bufs),
                      V(wo[:, :, dc * 128:(dc + 1) * 128], []))
        else:
            w = dsw_w_in.ap[j].rearrange("(k p) c -> p k c", p=128)
            for cg in range(18):
                P.dma("pool", V(sw_din.ap[j, cg].rearrange("p (k c) -> p k c", k=KC), sw_din.bufs),
                      V(w[:, :, cg * 512:(cg + 1) * 512], []))
            wo = dsw_w_out.ap[j].rearrange("(k p) c -> p k c", p=128)
            for dc in range(8):
                P.dma("pool", V(sw_dout.ap[j, dc].rearrange("p (k c) -> p k c", k=8), sw_dout.bufs),
                      V(wo[:, :, dc * 128:(dc + 1) * 128], []))
        wu = mlp_w_up.ap[i].rearrange("(k p) c -> p k c", p=128)
        for cg in range(8):
            P.dma("pool", V(sw_up.ap[i, cg].rearrange("p (k c) -> p k c", k=KC), sw_up.bufs),
                  V(wu[:, :, cg * 512:(cg + 1) * 512], []))
        wd = mlp_w_down.ap[i].rearrange("(k p) c -> p k c", p=128)
        for dc in range(8):
            P.dma("pool", V(sw_dn.ap[i, dc].rearrange("p (k c) -> p k c", k=32), sw_dn.bufs),
                  V(wd[:, :, dc * 128:(dc + 1) * 128], []))

    for i in range(DEPTH):
        cast_layer(i)

    NW = 2
    wring = [sb("wring%d" % i, [128, 4096], BF16) for i in range(NW)]
    wctr = [0]

    def wload(src_ap, src_bufs, n):
        w = wring[wctr[0] % NW]
        wctr[0] += 1
        P.dma("sp", w[:, 0:n], V(src_ap, src_bufs))
        return w[:, 0:n]

    xT = sb("xT", [128, KC, TT], F32)
    hT = sb("hT", [128, KC, TT], BF16)
    sq = [sb("sq%d" % i, [128, TT], BF16) for i in range(2)]
    rstd = sb("rstd", [128, TT], F32)
    lnt = sb("lnt", [128, TT], F32)
    xtok = None
    pre = [sb("pre%d" % i, [128, TT + 3], F32) for i in range(2)]
    cyt = es.enter_context(nc.sbuf_tensor("cyt", [128, 2 * TT], F32))
    cy_b = [Buf(), Buf()]
    cy = [V(cyt[:, i_ * TT:(i_ + 1) * TT], [cy_b[i_]]) for i_ in range(2)]
    xtok = [V(cyt[:, :], cy_b)] * 2
    carry = sb("carry", [128, 32, 3], F32)
    arena = es.enter_context(nc.sbuf_tensor("arena", [128, 32 * TT], BF16))
    qk_buf, vt_buf = Buf(), Buf()
    qkT = V(arena[:, 0:16 * TT].rearrange("p (a t) -> p a t", a=16), [qk_buf])
    vT = [sb("vT%d" % i, [128, TT], BF16) for i in range(2)]
    v_tok = V(arena[:, 16 * TT:32 * TT].rearrange("p (s h d) -> p s h d", s=4, h=16), [vt_buf])
    k_tok = sb("k_tok", [128, 4, 8, 128], BF16)
    szT = sb("szT", [128, 16, TT], BF16)
    aT = V(arena[:, :].rearrange("p (a t) -> p a t", a=32), [qk_buf, vt_buf])
    rl = [sb("rl%d" % i, [128, TT], BF16) for i in range(2)]
    ba = sb("ba", [128, 4, 32], F32)
    beta = sb("beta", [128, 4, 16], F32)
    nbeta = sb("nbeta", [128, 4, 16], F32)
    gg = sb("gg", [128, 4, 16], F32)
    t16a = sb("t16a", [128, 4, 16], F32)
    t16b = sb("t16b", [128, 4, 16], F32)
    gc2 = [sb("gc%d" % i_, [128, 16], F32) for i_ in range(2)]
    ngc2 = [sb("ngc%d" % i_, [128, 16], F32) for i_ in range(2)]
    egc2 = [sb("egc%d" % i_, [128, 16], F32) for i_ in range(2)]
    kds2 = [sb("kds%d" % i_, [128, 16], F32) for i_ in range(2)]
    glast2 = [sb("glast%d" % i_, [128, 16], F32) for i_ in range(2)]
    gcT2 = [sb("gcT%d" % i_, [16, 128], F32) for i_ in range(2)]
    S = [sb("S%d" % i, [128, 4, 128], F32) for i in range(4)]
    S_bf = [sb("S_bf%d" % i, [128, 4, 128], BF16) for i in range(4)]
    o_tok = [sb("o_tok%d" % i, [128, 4, 128], F32) for i in range(2)] * 2
    on_tok = [sb("on_tok%d" % i, [128, 4, 128], BF16) for i in range(2)] * 2
    osq = [sb("osq0", [128, 4, 128], BF16)] * 2
    oss = sb("oss", [128, 16], F32)
    olnt = sb("olnt", [128, 16], F32)
    orstd = sb("orstd", [128, 16], F32)
    def hgbuf(name, dt):
        l = [sb("%s%d" % (name, i), [128, 4, 128], dt) for i in range(2)]
        return l + l
    g_DT = hgbuf("g_DT", F32)
    g_DTm = hgbuf("g_DTm", BF16)
    g_DTs = hgbuf("g_DTs", BF16)
    g_att = hgbuf("g_att", BF16)
    att4 = g_att[0:2] + [sb("att4_%d" % i_, [128, 4, 128], BF16) for i_ in range(2)]
    gZ1 = hgbuf("gZ1", BF16)
    gZ2 = hgbuf("gZ2", BF16)
    g_P = [hgbuf("g_Pa", BF16), hgbuf("g_Pb", BF16)]
    g_PT = [hgbuf("g_PTa", BF16), hgbuf("g_PTb", BF16)]
    g_R = [hgbuf("g_Ra", BF16), hgbuf("g_Rb", BF16)]
    Rfin = g_R[0][0:2] + [sb("Rfin%d" % i_, [128, 4, 128], BF16) for i_ in range(2)]
    g_kg = hgbuf("g_kg", BF16)
    g_kd = hgbuf("g_kd", BF16)
    g_nwT = hgbuf("g_nwT", BF16)
    g_vn = hgbuf("g_vn", BF16)
    g_t = hgbuf("g_t", F32)
    preS = sb("preS", [128, 32, NS, 4], F32)
    scv = sb("scv", [96, 128], F32)
    prodS = V(g_t[0].ap.rearrange("p a b -> p (a b)").rearrange("p (c s j) -> p c s j", c=32, s=NS), g_t[0].bufs)
    cyS = sb("cyS", [128, 32, NS], F32)
    cySb = sb("cySb", [128, 32, NS], F32)
    outS = sb("outS", [128, 128], F32)

    def pre_norm(gain, lidx, x_in=xT):
        bk = bank()
        for kc in range(KC):
            s = sq[kc % 2]
            act(s, x_in[:, kc, :], AF.Square)
            mm(bk, ones_b, s, start=(kc == 0), stop=(kc == KC - 1))
        act(lnt, bk, AF.Ln, bias=EPS, scale=1.0 / D)
        act(rstd, lnt, AF.Exp, scale=-0.5)
        for kc in range(KC):
            stt("dve", hT[:, kc, :], x_in[:, kc, :], gain[:, lidx, kc:kc + 1], rstd,
                ALU.mult, ALU.mult)

    def load_x_tile(ti, first):
        if not first:
            P.dma("sp", xT, V(xs.ap[:, :, ti * TT:(ti + 1) * TT].rearrange("k p t -> p k t"), xs.bufs))
            return
        for sub in range(4):
            xt = xtok[sub % 2]
            if ti < NTP:
                P.dma("sp", xt, V(x_prompt.ap[ti * TT + sub * 128: ti * TT + (sub + 1) * 128, :], []))
            else:
                mset("pool", xt, 0.0)
                P.dma("sp", xt[0:1, :], V(x_sample.ap[sub:sub + 1, :], []))
            for half in range(2):
                bk = bank()
                for q in range(4):
                    kc = half * 4 + q
                    tr(bk[:, q * 128:(q + 1) * 128], xt[:, kc * 128:(kc + 1) * 128], ident_f)
                cp("act" if half == 0 else "dve",
                   xT[:, half * 4:(half + 1) * 4, sub * 128:(sub + 1) * 128],
                   bk.rr("p (q t) -> p q t", q=4))

    def store_x_tile(ti, last):
        if not last:
            P.dma("pool", V(xs.ap[:, :, ti * TT:(ti + 1) * TT].rearrange("k p t -> p k t"), xs.bufs), xT)
            return
        for sub in range(4):
            xt = xtok[sub % 2]
            for half in range(2):
                bk = bank()
                for q in range(4):
                    kc = half * 4 + q
                    tr(bk[:, q * 128:(q + 1) * 128], xT[:, kc, sub * 128:(sub + 1) * 128], ident_f)
                cp("act" if half == 0 else "dve", xt[:, half * 512:(half + 1) * 512], bk)
            if ti < NTP:
                P.dma("pool", V(y_prompt.ap[ti * TT + sub * 128: ti * TT + (sub + 1) * 128, :], y_prompt.bufs), xt)
            else:
                P.dma("pool", V(y_sample.ap[sub:sub + 1, :], y_sample.bufs), xt[0:1, :])

    def mlp(i):
        pre_norm(gmlp, i)
        for cg in range(8):
            w = wload(sw_up.ap[i, cg], sw_up.bufs, KC * 512)
            w3 = w.rr("p (k c) -> p k c", k=KC)
            for q in range(4):
                bk = bank()
                for kc in range(KC):
                    mm(bk, w3[:, kc, q * 128:(q + 1) * 128], hT[:, kc, :], start=(kc == 0), stop=(kc == KC - 1))
                r = rl[q % 2]
                act(r, bk, AF.Relu)
                tt("pool" if q % 2 == 0 else "dve", aT[:, cg * 4 + q, :], r, r, ALU.mult)
        for dc in range(8):
            w = wload(sw_dn.ap[i, dc], sw_dn.bufs, 32 * 128)
            w3 = w.rr("p (k c) -> p k c", k=32)
            bk = bank()
            for kc in range(32):
                mm(bk, w3[:, kc, :], aT[:, kc, :], start=(kc == 0), stop=(kc == 31))
            tt("dve", xT[:, dc, :], xT[:, dc, :], bk, ALU.add)

    def gdn_layer(i, ti):
        j = i // 2
        sample = ti >= NTP
        pre_norm(gmix, i)
        if sample:
            for s in range(NS):
                P.dma("sp", scv, V(state_conv.ap[j, s].rearrange("j (c p) -> (j c) p", p=128), []))
                bk = bank()
                tr(bk[:, 0:96], scv, ident_f[0:96, 0:96])
                cp("act", preS[:, :, s, 0:3], bk[:, 0:96].rr("p (j c) -> p c j", j=3))
            mset("pool", qkT, 0.0)
        elif ti == 0:
            mset("pool", carry, 0.0)
            for hg in range(4):
                mset("pool", S[hg], 0.0)
                mset("pool", S_bf[hg], 0.0)

        def v_transpose(c, vsrc):
            bk2 = bank()
            bb = bk2.cast(BF16)
            for sub in range(4):
                tr(bb[:, sub * 128:(sub + 1) * 128], vsrc[:, sub * 128:(sub + 1) * 128], ident_b)
            cp("dve", v_tok[:, :, c - 16, :], bb[:, 0:512].rr("p (s d) -> p s d", s=4))

        for cg in range(12):
            w = wload(sw_gin.ap[j, cg], sw_gin.bufs, KC * 512)
            w3 = w.rr("p (k c) -> p k c", k=KC)
            for q in range(4):
                c = cg * 4 + q
                bk = bank()
                for kc in range(KC):
                    mm(bk, w3[:, kc, q * 128:(q + 1) * 128], hT[:, kc, :], start=(kc == 0), stop=(kc == KC - 1))
                if c >= 32:
                    act(szT[:, c - 32, :], bk, AF.Silu)
                    continue
                if sample:
                    cp("act", preS[:, c, :, 3], bk[:, 0:TT:128])
                    continue
                p_ = pre[c % 2]
                y = cy[c % 2]
                cp("pool", p_[:, 0:3], carry[:, c, :])
                cp("act", p_[:, 3:TT + 3], bk)
                cp("pool", carry[:, c, :], p_[:, TT:TT + 3])
                ts("dve", y, p_[:, 3:TT + 3], convw[:, j, 3, c:c + 1], ALU.mult)
                for tap in range(3):
                    stt("dve", y, p_[:, tap:tap + TT], convw[:, j, tap, c:c + 1], y, ALU.mult, ALU.add)
                if c < 16:
                    act(qkT[:, c, :], y, AF.Silu)
                else:
                    v_ = vT[c % 2]
                    act(v_, y, AF.Silu)
                    v_transpose(c, v_)
        if sample:
            for s in range(NS):
                tt("pool", prodS[:, :, s, :], preS[:, :, s, :], convw[:, j].rr("p j c -> p c j"), ALU.mult)
            red("dve", cyS, prodS)
            act(cySb, cyS, AF.Silu)
            cp("pool", qkT[:, :, 0:TT:128], cySb[:, 0:16, :])
            for c in range(16, 32):
                v_ = vT[c % 2]
                mset("pool", v_, 0.0)
                cp("pool", v_[:, 0:TT:128], cySb[:, c, :])
                v_transpose(c, v_)
            for s in range(NS):
                P.dma("pool", V(s_conv.ap[j, s, 0:2, :], s_conv.bufs), V(state_conv.ap[j, s, 1:3, :], []))
                bk = bank()
                tr(bk[0:32, 0:128], preS[:, :, s, 3], ident_f)
                cp("act", outS[0:32, :], bk[0:32, 0:128])
                P.dma("pool", V(s_conv.ap[j, s, 2, :].rearrange("(c p) -> c p", p=128), s_conv.bufs), outS[0:32, :])
        elif ti == NTP - 1:
            for jj in range(3):
                bk = bank()
                tr(bk[0:32, 0:128], carry[:, :, jj], ident_f)
                cp("act", outS[0:32, :], bk[0:32, 0:128])
                P.dma("pool", V(p_conv.ap[j, jj, :].rearrange("(c p) -> c p", p=128), p_conv.bufs), outS[0:32, :])

        for c in range(16):
            s = sq[c % 2]
            act(s, qkT[:, c, :], AF.Square)
            bk = bank()
            mm(bk, ones_b, s)
            act(lnt, bk, AF.Ln, bias=EPS)
            act(rstd, lnt, AF.Exp, scale=-0.5)
            stt("dve", qkT[:, c, :], qkT[:, c, :], (128.0 ** -0.5) if c < 8 else 1.0, rstd, ALU.mult, ALU.mult)
        for hk in range(8):
            bk2 = bank()
            bb = bk2.cast(BF16)
            for sub in range(4):
                tr(bb[:, sub * 128:(sub + 1) * 128], qkT[:, 8 + hk, sub * 128:(sub + 1) * 128], ident_b)
            cp("act", k_tok[:, :, hk, :], bb[:, 0:512].rr("p (s d) -> p s d", s=4))

        w = wload(sw_gba.ap[j], sw_gba.bufs, KC * 32)
        w3 = w.rr("p (k c) -> p k c", k=KC)
        bk = bank()
        for sub in range(4):
            for kc in range(KC):
                mm(bk[:, sub * 32:(sub + 1) * 32], hT[:, kc, sub * 128:(sub + 1) * 128], w3[:, kc, :],
                   start=(kc == 0), stop=(kc == KC - 1))
        cp("act", ba, bk[:, 0:128].rr("p (s c) -> p s c", s=4))
        act(t16a, ba[:, :, 0:16], AF.Exp, scale=-1.0)
        ts("dve", t16a, t16a, 1.0, ALU.add)
        recip(beta, t16a)
        tt("dve", t16b, ba[:, :, 16:32], dtb[:, j, :].us(1).bc([128, 4, 16]), ALU.add)
        ts("dve", t16a, t16b, -1.0, ALU.mult)
        tt("dve", t16a, t16a, t16b, ALU.max)
        act(t16a, t16a, AF.Exp, scale=-1.0)
        act(t16a, t16a, AF.Ln, bias=1.0)
        stt("dve", gg, t16b, 0.0, t16a, ALU.max, ALU.add)
        tt("dve", gg, gg, nA[:, j, :].us(1).bc([128, 4, 16]), ALU.mult)
        if sample:
            ts("dve", beta, beta, tokmask[:, 0:1], ALU.mult)
            ts("dve", gg, gg, tokmask[:, 0:1], ALU.mult)
        ts("dve", nbeta, beta, -1.0, ALU.mult)

        def b4(v):
            return v.us(2).bc([128, 4, 128])

        def r4(v):
            return v.rr("p (a b) i -> p a b i", a=2)

        def prologue(ch):
            pc = ch % 2
            g_c = gg[:, ch, :]
            bk = bank()
            mm(bk[:, 0:16], Uincl_f, g_c)
            mm(bk[:, 16:32], Lstrict_f, g_c)
            mm(bk[0:16, 32:160], g_c, Uincl_f)
            cp("act", gc2[pc], bk[:, 0:16])
            act(egc2[pc], bk[:, 0:16], AF.Exp)
            act(kds2[pc], bk[:, 16:32], AF.Exp)
            cp("act", gcT2[pc], bk[0:16, 32:160])
            ts("dve", ngc2[pc], gc2[pc], -1.0, ALU.mult)
            bk = bank()
            mm(bk[:, 0:16], lastsel_f, egc2[pc])
            cp("act", glast2[pc], bk[:, 0:16])

        def stageA(ch, hp):
            pc = ch % 2
            cs = slice(ch * 128, (ch + 1) * 128)
            ngc, gcT = ngc2[pc], gcT2[pc]
            for hg in (2 * hp, 2 * hp + 1):
                hs = slice(4 * hg, 4 * hg + 4)
                bkA = bank()
                for q in range(4):
                    mm(bkA[:, q * 128:(q + 1) * 128], SEL[:, 4 * hg + q, :], gcT)
                tt("dve", g_DT[hg], bkA.rr("p (q i) -> p q i", q=4), b4(ngc[:, hs]), ALU.add)
                ts("dve", g_DT[hg], g_DT[hg], 0.0, ALU.min)
                act(g_DT[hg], g_DT[hg], AF.Exp)
                stt("dve", g_DTm[hg], g_DT[hg], 1.0, Uincl4, ALU.min, ALU.mult)
                stt("dve", g_DTs[hg], g_DT[hg], 1.0, Ustrict4, ALU.min, ALU.mult)
                tt("dve", g_DTs[hg], g_DTs[hg], b4(nbeta[:, ch, hs]), ALU.mult)
                yield
                bkB = bank()
                for a_ in range(2):
                    hk = 2 * hg + a_
                    mm(bkB[:, a_ * 128:(a_ + 1) * 128], qkT[:, 8 + hk, cs], qkT[:, 8 + hk, cs])
                    mm(bkB[:, (2 + a_) * 128:(3 + a_) * 128], qkT[:, 8 + hk, cs], qkT[:, hk, cs])
                kk = bkB[:, 0:256].rr("p (a i) -> p a i", a=2).us(2).bc([128, 2, 2, 128])
                qk = bkB[:, 256:512].rr("p (a i) -> p a i", a=2).us(2).bc([128, 2, 2, 128])
                tt("dve", r4(g_P[0][hg]), kk, r4(g_DTs[hg]), ALU.mult)
                tt("dve", r4(att4[hg]), qk, r4(g_DTm[hg]), ALU.mult)
                bkC = bank().cast(BF16)
                for q in range(4):
                    tr(bkC[:, q * 128:(q + 1) * 128], g_P[0][hg][:, q, :], ident_b)
                cp("act", g_PT[0][hg], bkC[:, 0:512].rr("p (q i) -> p q i", q=4))
                yield
            for hg in (2 * hp, 2 * hp + 1):
                Tl, Ru, Xn, XnT = g_PT[1][hg], Rfin[hg], g_P[1][hg], g_R[1][hg]
                tt("dve", Xn, g_PT[0][hg], lvl[:, 0, :].us(1).bc([128, 4, 128]), ALU.mult)
                tt("pool", XnT, g_P[0][hg], lvl[:, 1, :].us(1).bc([128, 4, 128]), ALU.mult)
                tt("dve", Tl, Xn, ident4_b, ALU.add)
                tt("pool", Ru, XnT, ident4_b, ALU.add)
            yield
            for k in range(2, 8):
                for hg in (2 * hp, 2 * hp + 1):
                    Tl, Ru, Xn, XnT = g_PT[1][hg], Rfin[hg], g_P[1][hg], g_R[1][hg]
                    Z1, Z2 = gZ1[hg], gZ2[hg]
                    last = (k == 7)
                    tt("dve", Xn, g_PT[0][hg], lvl[:, 2 * k - 2, :].us(1).bc([128, 4, 128]), ALU.mult)
                    if not last:
                        tt("pool", XnT, g_P[0][hg], lvl[:, 2 * k - 1, :].us(1).bc([128, 4, 128]), ALU.mult)
                        bk = bank()
                        for q in range(4):
                            mm(bk[:, q * 128:(q + 1) * 128], XnT[:, q, :], Tl[:, q, :])
                        cp("act", Z1, bk.rr("p (q i) -> p q i", q=4))
                    bk = bank()
                    for q in range(4):
                        mm(bk[:, q * 128:(q + 1) * 128], Xn[:, q, :], Ru[:, q, :])
                    cp("act" if last else "dve", Z2, bk.rr("p (q i) -> p q i", q=4))
                    yield
                    if not last:
                        bkT_ = bank()
                        for q in range(4):
                            mm(bkT_[:, q * 128:(q + 1) * 128], Ru[:, q, :], Z1[:, q, :])
                    bkR_ = bank()
                    for q in range(4):
                        mm(bkR_[:, q * 128:(q + 1) * 128], Tl[:, q, :], Z2[:, q, :])
                    if not last:
                        tt("dve", Tl, Tl, bkT_.rr("p (q i) -> p q i", q=4), ALU.add)
                    tt("dve", Ru, Ru, bkR_.rr("p (q i) -> p q i", q=4), ALU.add)
                    yield

        def stageB(ch, hp):
            pc = ch % 2
            cs = slice(ch * 128, (ch + 1) * 128)
            egc, kds, glast = egc2[pc], kds2[pc], glast2[pc]
            for hg in (2 * hp, 2 * hp + 1):
                hs = slice(4 * hg, 4 * hg + 4)
                if sample:
                    P.dma("sp", S[hg], V(state_gdn.ap[j, ch, 4 * hg:4 * hg + 4].rearrange("h k v -> k h v"), []))
                    cp("pool", S_bf[hg], S[hg])
                R = Rfin[hg]
                ks = k_tok[:, ch, 2 * hg:2 * hg + 2, :].us(2).bc([128, 2, 2, 128])
                tt("dve", r4(g_kg[hg]), ks, egc[:, hs].rr("p (a b) -> p a b", a=2).us(3).bc([128, 2, 2, 128]), ALU.mult)
                tt("pool", r4(g_kd[hg]), ks, kds[:, hs].rr("p (a b) -> p a b", a=2).us(3).bc([128, 2, 2, 128]), ALU.mult)
                bk = bank()
                for q in range(4):
                    mm(bk[:, q * 128:(q + 1) * 128], g_kg[hg][:, q, :], R[:, q, :])
                act(g_nwT[hg], bk.rr("p (q i) -> p q i", q=4), AF.Copy, scale=-1.0)
                yield
                bk = bank()
                for q in range(4):
                    mm(bk[:, q * 128:(q + 1) * 128], R[:, q, :], v_tok[:, ch, 4 * hg + q, :], start=True, stop=False)
                    mm(bk[:, q * 128:(q + 1) * 128], g_nwT[hg][:, q, :], S_bf[hg][:, q, :], start=False, stop=True)
                tt("dve", g_vn[hg], bk.rr("p (q i) -> p q i", q=4), b4(beta[:, ch, hs]), ALU.mult)
                bk1 = bank()
                for q in range(4):
                    mm(bk1[:, q * 128:(q + 1) * 128], qkT[:, 2 * hg + q // 2, cs], S_bf[hg][:, q, :])
                tt("pool", S[hg], S[hg], b4(glast[:, hs]), ALU.mult)
                yield
                bk2 = bank()
                for q in range(4):
                    mm(bk2[:, q * 128:(q + 1) * 128], att4[hg][:, q, :], g_vn[hg][:, q, :])
                cp("act", g_t[hg], bk2.rr("p (q i) -> p q i", q=4))
                tt("dve", o_tok[hg], bk1.rr("p (q i) -> p q i", q=4), b4(egc[:, hs]), ALU.mult)
                tt("dve", o_tok[hg], o_tok[hg], g_t[hg], ALU.add)
                bk = bank()
                for q in range(4):
                    mm(bk[:, q * 128:(q + 1) * 128], g_kd[hg][:, q, :], g_vn[hg][:, q, :])
                tt("dve", S[hg], S[hg], bk.rr("p (q i) -> p q i", q=4), ALU.add)
                cp("act", S_bf[hg], S[hg])
                yield
                act(osq[hg % 2], o_tok[hg], AF.Square)
                red("dve", oss[:, hs], osq[hg % 2])
                act(olnt[:, hs], oss[:, hs], AF.Ln, bias=EPS, scale=1.0 / 128)
                act(orstd[:, hs], olnt[:, hs], AF.Exp, scale=-0.5)
                tt("dve", on_tok[hg], o_tok[hg], b4(orstd[:, hs]), ALU.mult)
                bkC = bank().cast(BF16)
                for q in range(4):
                    tr(bkC[:, q * 128:(q + 1) * 128], on_tok[hg][:, q, :], ident_b)
                stt("dve", szT[:, hs, cs], bkC[:, 0:512].rr("p (q t) -> p q t", q=4), onorm[:, j:j + 1],
                    szT[:, hs, cs], ALU.mult, ALU.mult)
                if sample:
                    P.dma("pool", V(s_gdn.ap[j, ch, 4 * hg:4 * hg + 4].rearrange("h k v -> k h v"), s_gdn.bufs), S[hg])
                elif ti == NTP - 1 and ch == 3:
                    P.dma("pool", V(p_gdn.ap[j, 4 * hg:4 * hg + 4].rearrange("h k v -> k h v"), p_gdn.bufs), S[hg])
                yield

        units = [(ch, hp) for ch in range(4) for hp in range(2)]
        prologue(0)
        for _ in stageA(*units[0]):
            pass
        for k_ in range(len(units)):
            nxt = units[k_ + 1] if k_ + 1 < len(units) else None
            if nxt is not None and nxt[1] == 0:
                prologue(nxt[0])
            ga = stageA(*nxt) if nxt is not None else iter(())
            gb = stageB(*units[k_])
            da = db = False
            while not (da and db):
                if not da:
                    try:
                        next(ga)
                        next(ga)
                    except StopIteration:
                        da = True
                if not db:
                    try:
                        next(gb)
                    except StopIteration:
                        db = True
        for dc in range(8):
            w = wload(sw_gout.ap[j, dc], sw_gout.bufs, 16 * 128)
            w3 = w.rr("p (k c) -> p k c", k=16)
            bk = bank()
            for kc in range(16):
                mm(bk, w3[:, kc, :], szT[:, kc, :], start=(kc == 0), stop=(kc == 15))
            tt("dve", xT[:, dc, :], xT[:, dc, :], bk, ALU.add)

    AB = [qk_buf, vt_buf]

    def out_kv(g, jb, kvi, half, src, t0, sub, sample):
        win, dil = GROUPS[g]
        hs = slice(half * 4, half * 4 + 4)
        if sample:
            P.dma("pool", V(s_kv[g].ap[jb, sub, win - 1:win, kvi, hs, :], s_kv[g].bufs), src[0:1])
        else:
            w0 = T - min(win, T)
            if t0 >= w0:
                P.dma("pool", V(p_kv[g].ap[jb, t0 - w0:t0 - w0 + 128, kvi, hs, :], p_kv[g].bufs), src)

    def dswA(i, ti):
        jb = i // 2
        sample = ti >= NTP
        pre_norm(gmix, i)
        tok0 = ti * TT
        for cgi in range(18):
            g, typ, half = cgi // 6, (cgi % 6) // 2, cgi % 2
            w = wload(sw_din.ap[jb, cgi], sw_din.bufs, KC * 512)
            w3 = w.rr("p (k c) -> p k c", k=KC)
            if typ < 2:
                gt = gtile[cgi % 2]
                src = (dsw_q_norm if typ == 0 else dsw_k_norm).ap[jb, g:g + 1, :].partition_broadcast(128)
                P.dma("sp", gt, V(src, []))
                tTa = V(arena[:, (cgi % 2) * 2048:(cgi % 2 + 1) * 2048].rearrange("p (h t) -> p h t", h=4), [qk_buf])
            for sub in range(4):
                bk = bank()
                for kc in range(KC):
                    mm(bk, hT[:, kc, sub * 128:(sub + 1) * 128], w3[:, kc, :], start=(kc == 0), stop=(kc == KC - 1))
                bk4 = bk.rr("p (h d) -> p h d", h=4)
                t0 = tok0 + sub * 128
                if typ == 2:
                    vf = g_DT[sub % 2]
                    vb = g_att[sub % 2]
                    cp("act", vf, bk4)
                    cp("dve", vb, bk4)
                    P.dma("pool", V(V_s.ap[g, t0:t0 + 128, half * 4:half * 4 + 4, :], V_s.bufs), vb)
                    out_kv(g, jb, 1, half, vf, t0, sub, sample)
                else:
                    sqh = osq[0]
                    act(sqh, bk4, AF.Square)
                    red("dve", oss[:, 0:4], sqh)
                    act(olnt[:, 0:4], oss[:, 0:4], AF.Ln, bias=EPS, scale=1.0 / 128)
                    act(orstd[:, 0:4], olnt[:, 0:4], AF.Exp, scale=-0.5)
                    nf = o_tok[sub % 2]
                    nb = on_tok[sub % 2]
                    tt("dve", nf, bk4, orstd[:, 0:4].us(2).bc([128, 4, 128]), ALU.mult)
                    tt("dve", nf, nf, gt.us(1).bc([128, 4, 128]), ALU.mult)
                    if typ == 1:
                        out_kv(g, jb, 0, half, nf, t0, sub, sample)
                    cp("act", nb, nf)
                    bkT = bank().cast(BF16)
                    for q in range(4):
                        tr(bkT[:, q * 128:(q + 1) * 128], nb[:, q, :], ident_b)
                    cp("act", tTa[:, :, sub * 128:(sub + 1) * 128], bkT[:, 0:512].rr("p (h t) -> p h t", h=4))
            if typ < 2:
                dst = QT_s if typ == 0 else KT_s
                P.dma("pool", V(dst.ap[g, half * 4:half * 4 + 4, :, tok0:tok0 + TT].rearrange("h p t -> p h t"),
                                dst.bufs), tTa)

    def dswB(i, ti):
        jb = i // 2
        P.dma("sp", szT[:, 0:8, :], V(OT_s.ap[:, :, ti * TT:(ti + 1) * TT].rearrange("h p t -> p h t"), OT_s.bufs))
        for dc in range(8):
            w = wload(sw_dout.ap[jb, dc], sw_dout.bufs, 8 * 128)
            w3 = w.rr("p (k c) -> p k c", k=8)
            bk = bank()
            for kc in range(8):
                mm(bk, w3[:, kc, :], szT[:, kc, :], start=(kc == 0), stop=(kc == 7))
            tt("dve", xT[:, dc, :], xT[:, dc, :], bk, ALU.add)

    def attn(i):
        jb = i // 2
        scale = 128.0 ** -0.5
        bulk_copies(jb)
        acc_o = V(szT.ap.rearrange("p a t -> p (a t)").bitcast(F32), szT.bufs)
        acc_d = V(xT.ap.rearrange("p a t -> p (a t)"), xT.bufs)
        qTg = V(arena[:, 0:TX], AB)
        kTg = V(arena[:, TX:2 * TX], AB)
        Vm = V(arena[:, 2 * TX:2 * TX + T], AB)
        Vs = V(arena[:, 2 * TX + T:3 * TX].rearrange("p (c e) -> p c e", c=4), AB)
        KVc = V(hT.ap.rearrange("p a t -> p (a t)")[:, 0:12 * 256].rearrange("p (x a e) -> p x a e", x=12, a=2), hT.bufs)
        kTc = V(wring[1].ap[:, 0:12 * 128].rearrange("p (x e) -> p x e", x=12), wring[1].bufs)
        obf = wring[0]
        obS = rl[0]
        accS_o = V(cy[0].ap[:, 0:512].rearrange("p (s q) -> p s q", s=4), cy[0].bufs)
        accS_d = V(cy[1].ap[:, 0:512].rearrange("p (s q) -> p s q", s=4), cy[1].bufs)
        Ef = [rstd, lnt]
        Eb = sq
        bctr_ = [0]

        def block(kprev, kcur, q, vprev, vcur, dst_o, dst_d, first):
            n = bctr_[0]
            bctr_[0] += 1
            ef = Ef[n % 2][:, 0:256]
            eb = Eb[n % 2][:, 0:256]
            bkS = bank()
            if kprev is not None:
                mm(bkS[:, 0:128], kprev, q)
            mm(bkS[:, 128:256], kcur, q)
            if kprev is not None:
                act(ef, bkS[:, 0:256], AF.Exp, scale=scale)
                tt("pool", eb.rr("p (a q) -> p a q", a=2), ef.rr("p (a q) -> p a q", a=2), mask2, ALU.mult)
            else:
                act(ef[:, 128:256], bkS[:, 128:256], AF.Exp, scale=scale)
                tt("pool", eb[:, 128:256], ef[:, 128:256], mask2[:, 1, :], ALU.mult)
            def fin():
                bkO = bank()
                if kprev is not None:
                    mm(bkO[:, 0:128], vprev, eb[:, 0:128], start=True, stop=False)
                    mm(bkO[:, 0:128], vcur, eb[:, 128:256], start=False, stop=True)
                    mm(bkO[:, 128:256], ones_b, eb[:, 0:128], start=True, stop=False)
                    mm(bkO[:, 128:256], ones_b, eb[:, 128:256], start=False, stop=True)
                else:
                    mm(bkO[:, 0:128], vcur, eb[:, 128:256])
                    mm(bkO[:, 128:256], ones_b, eb[:, 128:256])
                if first:
                    cp("dve", dst_o, bkO[:, 0:128])
                    cp("dve", dst_d, bkO[:, 128:256])
                else:
                    tt("dve", dst_o, dst_o, bkO[:, 0:128], ALU.add)
                    tt("dve", dst_d, dst_d, bkO[:, 128:256], ALU.add)

            if pend[0] is not None:
                pend[0]()
            pend[0] = fin

        pend = [None]

        def flush():
            if pend[0] is not None:
                pend[0]()
                pend[0] = None

        for h in range(8):
            for s_ in range(NS):
                for g, (win, dil) in enumerate(GROUPS):
                    P.dma("pool", KVc[:, s_ * 3 + g], V(cache[g].ap[jb, s_, 0:win:dil, :, h, :], []))
            for x0 in range(0, 12, 4):
                bkT = bank().cast(BF16)
                for q in range(4):
                    tr(bkT[:, q * 128:(q + 1) * 128], KVc[:, x0 + q, 0, :], ident_b)
                cp("act", kTc[:, x0:x0 + 4, :], bkT[:, 0:512].rr("p (x e) -> p x e", x=4))
            for g, (win, dil) in enumerate(GROUPS):
                P.dma("sp", qTg, V(QT_s.ap[g, h], QT_s.bufs))
                P.dma("sp", kTg, V(KT_s.ap[g, h], KT_s.bufs))
                nbr = T // (128 * dil)
                Vm4 = Vm.rr("p (r n e) -> p r n e", r=dil, n=nbr)
                for r in range(dil):
                    P.dma("sp", Vm4[:, r], V(V_s.ap[g, 0:T, h, :].rearrange("(n i r) e -> r i n e", i=128, r=dil)[r],
                                           V_s.bufs))
                P.dma("sp", Vs, V(V_s.ap[g, T:TX, h, :].rearrange("(c i) e -> i c e", i=128), V_s.bufs))
                for r in range(dil):
                    for n in range(nbr):
                        lo = r + dil * 128 * n
                        qs = slice(lo, lo + dil * 127 + 1, dil)
                        if n > 0:
                            lp = lo - dil * 128
                            ps_ = slice(lp, lp + dil * 127 + 1, dil)
                            block(kTg[:, ps_], kTg[:, qs], qTg[:, qs], Vm4[:, r, n - 1, :], Vm4[:, r, n, :],
                                  acc_o[:, qs], acc_d[:, qs], g == 0)
                        else:
                            block(None, kTg[:, qs], qTg[:, qs], None, Vm4[:, r, n, :], acc_o[:, qs], acc_d[:, qs], g == 0)
                for s_ in range(NS):
                    qs = slice(T + s_ * 128, T + (s_ + 1) * 128)
                    block(kTc[:, s_ * 3 + g, :], kTg[:, qs], qTg[:, qs], KVc[:, s_ * 3 + g, 1, :], Vs[:, s_, :],
                          accS_o[:, s_, :], accS_d[:, s_, :], g == 0)
                flush()
            recip(acc_d[:, 0:T], acc_d[:, 0:T])
            tt("dve", obf[:, 0:T], acc_o[:, 0:T], acc_d[:, 0:T], ALU.mult)
            P.dma("pool", V(OT_s.ap[h, :, 0:T], OT_s.bufs), obf[:, 0:T])
            recip(accS_d, accS_d)
            tt("dve", obS[:, 0:512].rr("p (s q) -> p s q", s=4), accS_o, accS_d, ALU.mult)
            P.dma("pool", V(OT_s.ap[h, :, T:TX], OT_s.bufs), obS[:, 0:512])

    for seg in range(NLB + 1):
        for ti in range(NT):
            load_x_tile(ti, seg == 0)
            if seg >= 1:
                dswB(2 * seg - 1, ti)
                mlp(2 * seg - 1)
            if 2 * seg < DEPTH:
                gdn_layer(2 * seg, ti)
                mlp(2 * seg)
            if 2 * seg + 1 < DEPTH:
                dswA(2 * seg + 1, ti)
            store_x_tile(ti, seg == NLB)
        if 2 * seg + 1 < DEPTH:
            attn(2 * seg + 1)
    P.emit()
    return nc


def _cmask():
    m = np.zeros((14, 128, 128), np.float32)
    i = np.arange(128)[:, None]
    j = np.arange(128)[None, :]
    for k in range(1, 8):
        b = 1 << k
        low = ((i // b) == (j // b)) & ((i % b) >= b // 2) & ((j % b) < b // 2)
        m[2 * (k - 1)] = low
        m[2 * (k - 1) + 1] = low.T
    return m


def run(cfg, in_maps):
    for m in in_maps:
        m["cmask"] = _cmask()
    nc = build(cfg)
    return run_bass_kernel_spmd(nc, in_maps, core_ids=list(range(len(in_maps))))


def kernel(x_prompt, x_sample, state_gdn, state_conv, cache_kv_w128, cache_kv_w512, cache_kv_w2048,
           norm_mix, norm_mlp, gdn_w_in, gdn_conv_w, gdn_a_log, gdn_dt_bias, gdn_o_norm, gdn_w_out,
           dsw_w_in, dsw_q_norm, dsw_k_norm, dsw_w_out, mlp_w_up, mlp_w_down):
    f = lambda a: np.ascontiguousarray(np.asarray(a), dtype=np.float32)
    B, T, _ = x_prompt.shape
    DB = x_sample.shape[0]
    DEPTH = norm_mix.shape[0]
    ncores = 8
    cfg = {"T": T, "DEPTH": DEPTH}
    shared = {"norm_mix": f(norm_mix), "norm_mlp": f(norm_mlp), "gdn_w_in": f(gdn_w_in),
              "gdn_conv_w": f(gdn_conv_w), "gdn_a_log": f(gdn_a_log), "gdn_dt_bias": f(gdn_dt_bias),
              "gdn_o_norm": f(gdn_o_norm), "gdn_w_out": f(gdn_w_out), "mlp_w_up": f(mlp_w_up),
              "mlp_w_down": f(mlp_w_down), "dsw_w_in": f(dsw_w_in), "dsw_q_norm": f(dsw_q_norm),
              "dsw_k_norm": f(dsw_k_norm), "dsw_w_out": f(dsw_w_out)}
    in_maps = []
    for c in range(ncores):
        b = c % B
        sl = slice(c * NS, (c + 1) * NS)
        m = dict(shared)
        m["x_prompt"] = f(x_prompt[b])
        m["x_sample"] = f(x_sample[sl, 0])
        m["state_gdn"] = f(state_gdn[:, sl])
        m["state_conv"] = f(state_conv[:, sl])
        m["cache_kv_w128"] = f(cache_kv_w128[:, sl])
        m["cache_kv_w512"] = f(cache_kv_w512[:, sl])
        m["cache_kv_w2048"] = f(cache_kv_w2048[:, sl])
        in_maps.append(m)
    res = run(cfg, in_maps).results
    NLA = (DEPTH + 1) // 2
    NLB = DEPTH // 2
    y_prompt = np.stack([res[b]["y_prompt"] for b in range(B)]).astype(np.float32)
    y_sample = np.concatenate([res[c]["y_sample"] for c in range(ncores)])[:, None, :].astype(np.float32)
    p_gdn = np.stack([res[b]["p_gdn"] for b in range(B)], axis=1).astype(np.float32)
    p_conv = np.stack([res[b]["p_conv"] for b in range(B)], axis=1).astype(np.float32)
    s_gdn = np.concatenate([res[c]["s_gdn"] for c in range(ncores)], axis=1).astype(np.float32)
    s_conv = np.concatenate([res[c]["s_conv"] for c in range(ncores)], axis=1).astype(np.float32)
    p_kv = [np.stack([res[b]["p_kv%d" % g] for b in range(B)], axis=1).astype(np.float32) for g in range(3)]
    s_kv = [np.concatenate([res[c]["s_kv%d" % g] for c in range(ncores)], axis=1).astype(np.float32) for g in range(3)]
    return (y_prompt, y_sample, p_gdn, p_conv, p_kv[0], p_kv[1], p_kv[2], s_gdn, s_conv, s_kv[0], s_kv[1], s_kv[2])
```

```python
import contextlib
import numpy as np
import concourse.bass as bass
import concourse.mybir as mybir
from concourse.bass_utils import run_bass_kernel_spmd

F32 = mybir.dt.float32
BF16 = mybir.dt.bfloat16
ALU = mybir.AluOpType
AF = mybir.ActivationFunctionType
AX = mybir.AxisListType

D = 1024
KC = 8
EPS = 1e-6
NS = 4
TT = 512


class Buf:
    __slots__ = ("w", "rs", "const", "psum")

    def __init__(self, const=False, psum=False):
        self.w = None
        self.rs = []
        self.const = const
        self.psum = psum


class Ins:
    __slots__ = ("eng", "fn", "deps", "needed", "idx", "dma", "sem", "target", "prev")

    def __init__(self, eng, fn, deps, dma=False):
        self.eng = eng
        self.fn = fn
        self.deps = deps
        self.needed = False
        self.idx = 0
        self.dma = dma
        self.sem = None
        self.target = 0
        self.prev = None


class V:
    def __init__(self, ap, bufs):
        self.ap = ap
        self.bufs = bufs

    def __getitem__(self, k):
        return V(self.ap[k], self.bufs)

    def rr(self, pat, **kw):
        return V(self.ap.rearrange(pat, **kw), self.bufs)

    def bc(self, shape):
        return V(self.ap.to_broadcast(list(shape)), self.bufs)

    def us(self, ax):
        return V(self.ap.unsqueeze(ax), self.bufs)

    def cast(self, dt):
        return V(self.ap.bitcast(dt), self.bufs)


class Prog:
    def __init__(self, nc, es):
        self.nc = nc
        self.es = es
        self.ins = []
        self.E = {"pe": nc.tensor, "act": nc.scalar, "dve": nc.vector, "pool": nc.gpsimd, "sp": nc.sync}
        self.engsem = {k: es.enter_context(nc.semaphore("es_" + k)) for k in ("pe", "act", "dve", "pool")}
        self.dsems = {q: [es.enter_context(nc.semaphore("ds_%s%d" % (q, i))) for i in range(n)]
                      for q, n in (("sp", 14), ("pool", 10), ("act", 4))}
        self.dcount = {q: 0 for q in self.dsems}
        self.dlast = {}
        self.dtot = {}

    def _deps(self, r, w):
        deps = {}
        for b in r:
            if b.psum:
                continue
            if b.w is not None:
                deps[id(b.w)] = (b.w, "RAW")
        for b in list(w) + [b for b in r if b.psum]:
            if b.w is not None and id(b.w) not in deps:
                deps[id(b.w)] = (b.w, "RAW" if b.psum else "WAW")
            for x in b.rs:
                if id(x) not in deps:
                    deps[id(x)] = (x, "WAR")
        return list(deps.values())

    def _commit(self, ins, r, w):
        for b in r:
            if b.psum:
                b.w = ins
                b.rs = []
            elif not b.const:
                b.rs.append(ins)
        for b in w:
            b.w = ins
            b.rs = []
        self.ins.append(ins)

    def op(self, eng, fn, r=(), w=()):
        ins = Ins(eng, fn, self._deps(r, w))
        self._commit(ins, r, w)
        return ins

    def dma(self, q, out, in_, own_sem=False, **kw):
        E = self.E[q]
        o, i = out.ap, in_.ap
        ins = Ins(q, (lambda: E.dma_start(out=o, in_=i, **kw)), self._deps(in_.bufs, out.bufs), dma=True)
        if own_sem:
            sem = self.es.enter_context(self.nc.semaphore("bulk%d" % len(self.dtot)))
        else:
            n = self.dcount[q]
            self.dcount[q] = n + 1
            sem = self.dsems[q][n % len(self.dsems[q])]
        ins.sem = sem
        ins.prev = self.dlast.get(id(sem))
        ins.target = (ins.prev.target if ins.prev else 0) + 16
        self.dlast[id(sem)] = ins
        self.dtot[id(sem)] = (q, sem, ins.target)
        self._commit(ins, in_.bufs, out.bufs)
        return ins

    def emit(self):
        def need(c, d, kind):
            if d.dma:
                return True
            if c.eng == d.eng:
                if c.dma:
                    return True
                if c.eng == "pe":
                    return False
                return kind == "RAW"
            return True

        cnt = {k: 0 for k in self.engsem}
        for c in self.ins:
            for d, kind in c.deps:
                if need(c, d, kind):
                    d.needed = True
        for c in self.ins:
            if (not c.dma) and c.needed:
                cnt[c.eng] += 1
                c.idx = cnt[c.eng]
        waited = {k: {} for k in self.E}
        for c in self.ins:
            E = self.E[c.eng]
            wl = {}
            for d, kind in c.deps:
                if not need(c, d, kind):
                    continue
                if d.dma:
                    s, v = d.sem, d.target
                else:
                    s, v = self.engsem[d.eng], d.idx
                if wl.get(id(s), (None, 0))[1] < v:
                    wl[id(s)] = (s, v)
            if c.dma and c.prev is not None:
                s, v = c.sem, c.prev.target
                if wl.get(id(s), (None, 0))[1] < v:
                    wl[id(s)] = (s, v)
            for s, v in wl.values():
                if waited[c.eng].get(id(s), 0) < v:
                    E.wait_ge(s, v)
                    waited[c.eng][id(s)] = v
            bi = c.fn()
            if c.dma:
                bi.then_inc(c.sem, 16)
            elif c.needed:
                bi.then_inc(self.engsem[c.eng], 1)
        for q, sem, tot in self.dtot.values():
            if waited[q].get(id(sem), 0) < tot:
                self.E[q].wait_ge(sem, tot)
        self.ins = []


def build(cfg):
    T = cfg["T"]
    DEPTH = cfg["DEPTH"]
    NLA = (DEPTH + 1) // 2
    NLB = DEPTH // 2
    NTP = T // TT
    NT = NTP + 1
    TX = T + TT
    nc = bass.Bass("TRN2", target_bir_lowering=False)
    es = contextlib.ExitStack()
    P = Prog(nc, es)
    E = P.E

    def dram(name, shape, dt, kind):
        return V(nc.dram_tensor(name, list(shape), dt, kind=kind).ap(), [Buf()])

    def dram_in(name, shape):
        v = dram(name, shape, F32, "ExternalInput")
        v.bufs = []
        return v

    x_prompt = dram_in("x_prompt", [T, D])
    x_sample = dram_in("x_sample", [NS, D])
    state_gdn = dram_in("state_gdn", [max(NLA, 1), NS, 16, 128, 128])
    state_conv = dram_in("state_conv", [max(NLA, 1), NS, 3, 4096])
    norm_mix = dram_in("norm_mix", [DEPTH, D])
    norm_mlp = dram_in("norm_mlp", [DEPTH, D])
    gdn_w_in = dram_in("gdn_w_in", [max(NLA, 1), D, 6176])
    gdn_conv_w = dram_in("gdn_conv_w", [max(NLA, 1), 4, 4096])
    gdn_a_log = dram_in("gdn_a_log", [max(NLA, 1), 16])
    gdn_dt_bias = dram_in("gdn_dt_bias", [max(NLA, 1), 16])
    gdn_o_norm = dram_in("gdn_o_norm", [max(NLA, 1), 128])
    gdn_w_out = dram_in("gdn_w_out", [max(NLA, 1), 2048, D])
    mlp_w_up = dram_in("mlp_w_up", [DEPTH, D, 4096])
    mlp_w_down = dram_in("mlp_w_down", [DEPTH, 4096, D])

    cmask = dram_in("cmask", [14, 128, 128])
    y_prompt = dram("y_prompt", [T, D], F32, "ExternalOutput")
    y_sample = dram("y_sample", [NS, D], F32, "ExternalOutput")
    p_gdn = dram("p_gdn", [max(NLA, 1), 16, 128, 128], F32, "ExternalOutput")
    p_conv = dram("p_conv", [max(NLA, 1), 3, 4096], F32, "ExternalOutput")
    s_gdn = dram("s_gdn", [max(NLA, 1), NS, 16, 128, 128], F32, "ExternalOutput")
    s_conv = dram("s_conv", [max(NLA, 1), NS, 3, 4096], F32, "ExternalOutput")

    GROUPS = ((128, 1), (512, 4), (2048, 16))
    if NLB:
        cache = [dram_in("cache_kv_w%d" % w_, [NLB, NS, w_, 2, 8, 128]) for w_, _ in GROUPS]
        dsw_w_in = dram_in("dsw_w_in", [NLB, D, 9216])
        dsw_q_norm = dram_in("dsw_q_norm", [NLB, 3, 128])
        dsw_k_norm = dram_in("dsw_k_norm", [NLB, 3, 128])
        dsw_w_out = dram_in("dsw_w_out", [NLB, 1024, D])
        p_kv = [dram("p_kv%d" % g_, [NLB, min(w_, T), 2, 8, 128], F32, "ExternalOutput")
                for g_, (w_, _) in enumerate(GROUPS)]
        s_kv = [dram("s_kv%d" % g_, [NLB, NS, w_, 2, 8, 128], F32, "ExternalOutput")
                for g_, (w_, _) in enumerate(GROUPS)]
        sw_din = dram("sw_din", [NLB, 18, 128, KC * 512], BF16, "Internal")
        sw_dout = dram("sw_dout", [NLB, 8, 128, 8 * 128], BF16, "Internal")
        QT_s = dram("QT_s", [3, 8, 128, TX], BF16, "Internal")
        KT_s = dram("KT_s", [3, 8, 128, TX], BF16, "Internal")
        V_s = dram("V_s", [3, TX, 8, 128], BF16, "Internal")
        OT_s = dram("OT_s", [8, 128, TX], BF16, "Internal")

    def bulk_copies(jb_):
        for g_, (w_, _) in enumerate(GROUPS):
            for s_ in range(NS):
                P.dma("sp", V(s_kv[g_].ap[jb_, s_, 0:w_ - 1].rearrange("r a h d -> r (a h d)"), s_kv[g_].bufs),
                      V(cache[g_].ap[jb_, s_, 1:w_].rearrange("r a h d -> r (a h d)"), []), own_sem=True)

    sw_gin = dram("sw_gin", [max(NLA, 1), 12, 128, KC * 512], BF16, "Internal")
    sw_gba = dram("sw_gba", [max(NLA, 1), 128, KC * 32], BF16, "Internal")
    sw_gout = dram("sw_gout", [max(NLA, 1), 8, 128, 16 * 128], BF16, "Internal")
    sw_up = dram("sw_up", [DEPTH, 8, 128, KC * 512], BF16, "Internal")
    sw_dn = dram("sw_dn", [DEPTH, 8, 128, 32 * 128], BF16, "Internal")
    xs = dram("xs", [KC, 128, TX], F32, "Internal")

    def sb(name, shape, dt, const=False):
        t = es.enter_context(nc.sbuf_tensor(name, list(shape), dt))
        return V(t[:], [Buf(const=const)])

    banks = [V(es.enter_context(nc.psum_tensor("bank%d" % i, [128, 512], F32))[:], [Buf(psum=True)])
             for i in range(8)]
    bctr = [0]

    def bank():
        b = banks[bctr[0] % 8]
        bctr[0] += 1
        return b

    def mm(out, lhsT, rhs, start=True, stop=True):
        o, l, r = out.ap, lhsT.ap, rhs.ap
        P.op("pe", lambda: nc.tensor.matmul(o, lhsT=l, rhs=r, start=start, stop=stop),
             r=lhsT.bufs + rhs.bufs, w=out.bufs)

    def tr(out, in_, ident):
        o, i, d = out.ap, in_.ap, ident.ap
        P.op("pe", lambda: nc.tensor.transpose(o, i, d), r=in_.bufs + ident.bufs, w=out.bufs)

    def act(out, in_, func, bias=0.0, scale=1.0):
        o, i = out.ap, in_.ap
        P.op("act", lambda: nc.scalar.activation(out=o, in_=i, func=func, bias=bias, scale=scale),
             r=in_.bufs, w=out.bufs)

    def tt(eng, out, a, b, op):
        o, x, y = out.ap, a.ap, b.ap
        P.op(eng, lambda: E[eng].tensor_tensor(out=o, in0=x, in1=y, op=op), r=a.bufs + b.bufs, w=out.bufs)

    def ts(eng, out, a, s1, op0, s2=None, op1=None):
        o, x = out.ap, a.ap
        rb = list(a.bufs)
        if isinstance(s1, V):
            rb += s1.bufs
            s1 = s1.ap
        if isinstance(s2, V):
            rb += s2.bufs
            s2 = s2.ap
        if op1 is None:
            P.op(eng, lambda: E[eng].tensor_scalar(out=o, in0=x, scalar1=s1, scalar2=None, op0=op0), r=rb, w=out.bufs)
        else:
            P.op(eng, lambda: E[eng].tensor_scalar(out=o, in0=x, scalar1=s1, scalar2=s2, op0=op0, op1=op1),
                 r=rb, w=out.bufs)

    def stt(eng, out, a, s, b, op0, op1):
        o, x, y = out.ap, a.ap, b.ap
        rb = a.bufs + b.bufs
        if isinstance(s, V):
            rb = rb + s.bufs
            s = s.ap
        P.op(eng, lambda: E[eng].scalar_tensor_tensor(out=o, in0=x, scalar=s, in1=y, op0=op0, op1=op1),
             r=rb, w=out.bufs)

    def cp(eng, out, in_):
        o, i = out.ap, in_.ap
        if eng == "act":
            P.op("act", lambda: nc.scalar.copy(out=o, in_=i), r=in_.bufs, w=out.bufs)
        else:
            P.op(eng, lambda: E[eng].tensor_copy(out=o, in_=i), r=in_.bufs, w=out.bufs)

    def mset(eng, out, val):
        o = out.ap
        P.op(eng, lambda: E[eng].memset(o, val), r=[], w=out.bufs)

    def red(eng, out, in_, op=ALU.add):
        o, i = out.ap, in_.ap
        P.op(eng, lambda: E[eng].tensor_reduce(out=o, in_=i, axis=AX.X, op=op), r=in_.bufs, w=out.bufs)

    def recip(out, in_):
        o, i = out.ap, in_.ap
        P.op("dve", lambda: nc.vector.reciprocal(out=o, in_=i), r=in_.bufs, w=out.bufs)

    def asel(out, pattern, cmp, base, cm):
        o = out.ap
        P.op("pool", lambda: nc.gpsimd.affine_select(out=o, in_=o, pattern=pattern, compare_op=cmp, fill=0.0,
                                                     base=base, channel_multiplier=cm), r=out.bufs, w=out.bufs)

    ident_f = sb("ident_f", [128, 128], F32, True)
    ident_b = sb("ident_b", [128, 128], BF16, True)
    ones_b = sb("ones_b", [128, 128], BF16, True)
    Uincl_f = sb("Uincl_f", [128, 128], F32, True)
    Lstrict_f = sb("Lstrict_f", [128, 128], F32, True)
    lastsel_f = sb("lastsel_f", [128, 128], F32, True)
    Uincl4 = sb("Uincl4", [128, 4, 128], F32, True)
    Ustrict4 = sb("Ustrict4", [128, 4, 128], F32, True)
    ident4_b = sb("ident4_b", [128, 4, 128], BF16, True)
    SEL = sb("SEL", [16, 16, 128], F32, True)
    tokmask = sb("tokmask", [128, 1], F32, True)
    mask2 = sb("mask2", [128, 2, 128], F32, True)
    gtile = [sb("gtile%d" % i_, [128, 128], F32) for i_ in range(2)]

    mset("pool", ident_f, 1.0)
    asel(ident_f, [[-1, 128]], ALU.is_equal, 0, 1)
    cp("pool", ident_b, ident_f)
    mset("pool", ones_b, 1.0)
    mset("pool", Uincl_f, 1.0)
    asel(Uincl_f, [[1, 128]], ALU.is_ge, 0, -1)
    mset("pool", Lstrict_f, 1.0)
    asel(Lstrict_f, [[-1, 128]], ALU.is_gt, 0, 1)
    mset("pool", lastsel_f, 1.0)
    asel(lastsel_f, [[0, 128]], ALU.is_equal, -127, 1)
    mset("pool", Uincl4, 1.0)
    asel(Uincl4, [[0, 4], [1, 128]], ALU.is_ge, 0, -1)
    mset("pool", Ustrict4, 1.0)
    asel(Ustrict4, [[0, 4], [1, 128]], ALU.is_gt, 0, -1)
    for q_ in range(4):
        cp("pool", ident4_b[:, q_, :], ident_f)
    mset("pool", mask2, 1.0)
    asel(mask2[:, 0, :], [[-1, 128]], ALU.is_ge, 0, 1)
    asel(mask2[:, 1, :], [[1, 128]], ALU.is_ge, 0, -1)
    mset("pool", SEL, 1.0)
    asel(SEL, [[-1, 16], [0, 128]], ALU.is_equal, 0, 1)
    mset("pool", tokmask, 1.0)
    asel(tokmask, [[0, 1]], ALU.is_equal, 0, 1)

    lvl = sb("lvl", [128, 14, 128], BF16, True)
    P.dma("pool", lvl, V(cmask.ap.rearrange("m p f -> p m f"), []))
    gmix = sb("gmix", [128, DEPTH, KC], F32, True)
    gmlp = sb("gmlp", [128, DEPTH, KC], F32, True)
    P.dma("sp", gmix, V(norm_mix.ap.rearrange("l (k p) -> p l k", p=128), []), allow_slow_non_contiguous=True)
    P.dma("sp", gmlp, V(norm_mlp.ap.rearrange("l (k p) -> p l k", p=128), []), allow_slow_non_contiguous=True)
    if NLA:
        convw = sb("convw", [128, NLA, 4, 32], F32, True)
        P.dma("sp", convw, V(gdn_conv_w.ap.rearrange("l j (c p) -> p l j c", p=128), []),
              allow_slow_non_contiguous=True)
        onorm = sb("onorm", [128, NLA], F32, True)
        P.dma("sp", onorm, V(gdn_o_norm.ap.rearrange("l p -> p l"), []), allow_slow_non_contiguous=True)
        nA = sb("nA", [128, NLA, 16], F32, True)
        dtb = sb("dtb", [128, NLA, 16], F32, True)
        for l in range(NLA):
            P.dma("sp", nA[:, l, :], V(gdn_a_log.ap[l:l + 1, :].partition_broadcast(128), []))
            P.dma("sp", dtb[:, l, :], V(gdn_dt_bias.ap[l:l + 1, :].partition_broadcast(128), []))
        act(nA, nA, AF.Exp)
        ts("dve", nA, nA, -1.0, ALU.mult)

    def cast_layer(i):
        j = i // 2
        if i % 2 == 0:
            w = gdn_w_in.ap[j].rearrange("(k p) c -> p k c", p=128)
            for cg in range(12):
                P.dma("pool", V(sw_gin.ap[j, cg].rearrange("p (k c) -> p k c", k=KC), sw_gin.bufs),
                      V(w[:, :, cg * 512:(cg + 1) * 512], []))
            P.dma("pool", V(sw_gba.ap[j].rearrange("p (k c) -> p k c", k=KC), sw_gba.bufs),
                  V(w[:, :, 6144:6176], []))
            wo = gdn_w_out.ap[j].rearrange("(k p) c -> p k c", p=128)
            for dc in range(8):
                P.dma("pool", V(sw_gout.ap[j, dc].rearrange("p (k c) -> p k c", k=16), sw_gout.bufs),
                      V(wo[:, :, dc * 128:(dc + 1) * 128], []))
        else:
            w = dsw_w_in.ap[j].rearrange("(k p) c -> p k c", p=128)
            for cg in range(18):
                P.dma("pool", V(sw_din.ap[j, cg].rearrange("p (k c) -> p k c", k=KC), sw_din.bufs),
                      V(w[:, :, cg * 512:(cg + 1) * 512], []))
            wo = dsw_w_out.ap[j].rearrange("(k p) c -> p k c", p=128)
            for dc in range(8):
                P.dma("pool", V(sw_dout.ap[j, dc].rearrange("p (k c) -> p k c", k=8), sw_dout.bufs),
                      V(wo[:, :, dc * 128:(dc + 1) * 128], []))
        wu = mlp_w_up.ap[i].rearrange("(k p) c -> p k c", p=128)
        for cg in range(8):
            P.dma("pool", V(sw_up.ap[i, cg].rearrange("p (k c) -> p k c", k=KC), sw_up.bufs),
                  V(wu[:, :, cg * 512:(cg + 1) * 512], []))
        wd = mlp_w_down.ap[i].rearrange("(k p) c -> p k c", p=128)
        for dc in range(8):
            P.dma("pool", V(sw_dn.ap[i, dc].rearrange("p (k c) -> p k c", k=32), sw_dn.bufs),
                  V(wd[:, :, dc * 128:(dc + 1) * 128], []))

    for i in range(DEPTH):
        cast_layer(i)

    NW = 2
    wring = [sb("wring%d" % i, [128, 4096], BF16) for i in range(NW)]
    wctr = [0]

    def wload(src_ap, src_bufs, n):
        w = wring[wctr[0] % NW]
        wctr[0] += 1
        P.dma("sp", w[:, 0:n], V(src_ap, src_bufs))
        return w[:, 0:n]

    xT = sb("xT", [128, KC, TT], F32)
    hT = sb("hT", [128, KC, TT], BF16)
    sq = [sb("sq%d" % i, [128, TT], BF16) for i in range(2)]
    rstd = sb("rstd", [128, TT], F32)
    lnt = sb("lnt", [128, TT], F32)
    xtok = None
    pre = [sb("pre%d" % i, [128, TT + 3], F32) for i in range(2)]
    cyt = es.enter_context(nc.sbuf_tensor("cyt", [128, 2 * TT], F32))
    cy_b = [Buf(), Buf()]
    cy = [V(cyt[:, i_ * TT:(i_ + 1) * TT], [cy_b[i_]]) for i_ in range(2)]
    xtok = [V(cyt[:, :], cy_b)] * 2
    carry = sb("carry", [128, 32, 3], F32)
    arena = es.enter_context(nc.sbuf_tensor("arena", [128, 32 * TT], BF16))
    qk_buf, vt_buf = Buf(), Buf()
    qkT = V(arena[:, 0:16 * TT].rearrange("p (a t) -> p a t", a=16), [qk_buf])
    vT = [sb("vT%d" % i, [128, TT], BF16) for i in range(2)]
    v_tok = V(arena[:, 16 * TT:32 * TT].rearrange("p (s h d) -> p s h d", s=4, h=16), [vt_buf])
    k_tok = sb("k_tok", [128, 4, 8, 128], BF16)
    szT = sb("szT", [128, 16, TT], BF16)
    aT = V(arena[:, :].rearrange("p (a t) -> p a t", a=32), [qk_buf, vt_buf])
    rl = [sb("rl%d" % i, [128, TT], BF16) for i in range(2)]
    ba = sb("ba", [128, 4, 32], F32)
    beta = sb("beta", [128, 4, 16], F32)
    nbeta = sb("nbeta", [128, 4, 16], F32)
    gg = sb("gg", [128, 4, 16], F32)
    t16a = sb("t16a", [128, 4, 16], F32)
    t16b = sb("t16b", [128, 4, 16], F32)
    gc2 = [sb("gc%d" % i_, [128, 16], F32) for i_ in range(2)]
    ngc2 = [sb("ngc%d" % i_, [128, 16], F32) for i_ in range(2)]
    egc2 = [sb("egc%d" % i_, [128, 16], F32) for i_ in range(2)]
    kds2 = [sb("kds%d" % i_, [128, 16], F32) for i_ in range(2)]
    glast2 = [sb("glast%d" % i_, [128, 16], F32) for i_ in range(2)]
    gcT2 = [sb("gcT%d" % i_, [16, 128], F32) for i_ in range(2)]
    S = [sb("S%d" % i, [128, 4, 128], F32) for i in range(4)]
    S_bf = [sb("S_bf%d" % i, [128, 4, 128], BF16) for i in range(4)]
    o_tok = [sb("o_tok%d" % i, [128, 4, 128], F32) for i in range(2)] * 2
    on_tok = [sb("on_tok%d" % i, [128, 4, 128], BF16) for i in range(2)] * 2
    osq = [sb("osq0", [128, 4, 128], BF16)] * 2
    oss = sb("oss", [128, 16], F32)
    olnt = sb("olnt", [128, 16], F32)
    orstd = sb("orstd", [128, 16], F32)
    def hgbuf(name, dt):
        l = [sb("%s%d" % (name, i), [128, 4, 128], dt) for i in range(2)]
        return l + l
    g_DT = hgbuf("g_DT", F32)
    g_DTm = hgbuf("g_DTm", BF16)
    g_DTs = hgbuf("g_DTs", BF16)
    g_att = hgbuf("g_att", BF16)
    att4 = g_att[0:2] + [sb("att4_%d" % i_, [128, 4, 128], BF16) for i_ in range(2)]
    gZ1 = hgbuf("gZ1", BF16)
    gZ2 = hgbuf("gZ2", BF16)
    g_P = [hgbuf("g_Pa", BF16), hgbuf("g_Pb", BF16)]
    g_PT = [hgbuf("g_PTa", BF16), hgbuf("g_PTb", BF16)]
    g_R = [hgbuf("g_Ra", BF16), hgbuf("g_Rb", BF16)]
    Rfin = g_R[0][0:2] + [sb("Rfin%d" % i_, [128, 4, 128], BF16) for i_ in range(2)]
    g_kg = hgbuf("g_kg", BF16)
    g_kd = hgbuf("g_kd", BF16)
    g_nwT = hgbuf("g_nwT", BF16)
    g_vn = hgbuf("g_vn", BF16)
    g_t = hgbuf("g_t", F32)
    preS = sb("preS", [128, 32, NS, 4], F32)
    scv = sb("scv", [96, 128], F32)
    prodS = V(g_t[0].ap.rearrange("p a b -> p (a b)").rearrange("p (c s j) -> p c s j", c=32, s=NS), g_t[0].bufs)
    cyS = sb("cyS", [128, 32, NS], F32)
    cySb = sb("cySb", [128, 32, NS], F32)
    outS = sb("outS", [128, 128], F32)

    def pre_norm(gain, lidx, x_in=xT):
        bk = bank()
        for kc in range(KC):
            s = sq[kc % 2]
            act(s, x_in[:, kc, :], AF.Square)
            mm(bk, ones_b, s, start=(kc == 0), stop=(kc == KC - 1))
        act(lnt, bk, AF.Ln, bias=EPS, scale=1.0 / D)
        act(rstd, lnt, AF.Exp, scale=-0.5)
        for kc in range(KC):
            stt("dve", hT[:, kc, :], x_in[:, kc, :], gain[:, lidx, kc:kc + 1], rstd,
                ALU.mult, ALU.mult)

    def load_x_tile(ti, first):
        if not first:
            P.dma("sp", xT, V(xs.ap[:, :, ti * TT:(ti + 1) * TT].rearrange("k p t -> p k t"), xs.bufs))
            return
        for sub in range(4):
            xt = xtok[sub % 2]
            if ti < NTP:
                P.dma("sp", xt, V(x_prompt.ap[ti * TT + sub * 128: ti * TT + (sub + 1) * 128, :], []))
            else:
                mset("pool", xt, 0.0)
                P.dma("sp", xt[0:1, :], V(x_sample.ap[sub:sub + 1, :], []))
            for half in range(2):
                bk = bank()
                for q in range(4):
                    kc = half * 4 + q
                    tr(bk[:, q * 128:(q + 1) * 128], xt[:, kc * 128:(kc + 1) * 128], ident_f)
                cp("act" if half == 0 else "dve",
                   xT[:, half * 4:(half + 1) * 4, sub * 128:(sub + 1) * 128],
                   bk.rr("p (q t) -> p q t", q=4))

    def store_x_tile(ti, last):
        if not last:
            P.dma("pool", V(xs.ap[:, :, ti * TT:(ti + 1) * TT].rearrange("k p t -> p k t"), xs.bufs), xT)
            return
        for sub in range(4):
            xt = xtok[sub % 2]
            for half in range(2):
                bk = bank()
                for q in range(4):
                    kc = half * 4 + q
                    tr(bk[:, q * 128:(q + 1) * 128], xT[:, kc, sub * 128:(sub + 1) * 128], ident_f)
                cp("act" if half == 0 else "dve", xt[:, half * 512:(half + 1) * 512], bk)
            if ti < NTP:
                P.dma("pool", V(y_prompt.ap[ti * TT + sub * 128: ti * TT + (sub + 1) * 128, :], y_prompt.bufs), xt)
            else:
                P.dma("pool", V(y_sample.ap[sub:sub + 1, :], y_sample.bufs), xt[0:1, :])

    def mlp(i):
        pre_norm(gmlp, i)
        for cg in range(8):
            w = wload(sw_up.ap[i, cg], sw_up.bufs, KC * 512)
            w3 = w.rr("p (k c) -> p k c", k=KC)
            for q in range(4):
                bk = bank()
                for kc in range(KC):
                    mm(bk, w3[:, kc, q * 128:(q + 1) * 128], hT[:, kc, :], start=(kc == 0), stop=(kc == KC - 1))
                r = rl[q % 2]
                act(r, bk, AF.Relu)
                tt("pool" if q % 2 == 0 else "dve", aT[:, cg * 4 + q, :], r, r, ALU.mult)
        for dc in range(8):
            w = wload(sw_dn.ap[i, dc], sw_dn.bufs, 32 * 128)
            w3 = w.rr("p (k c) -> p k c", k=32)
            bk = bank()
            for kc in range(32):
                mm(bk, w3[:, kc, :], aT[:, kc, :], start=(kc == 0), stop=(kc == 31))
            tt("dve", xT[:, dc, :], xT[:, dc, :], bk, ALU.add)

    def gdn_layer(i, ti):
        j = i // 2
        sample = ti >= NTP
        pre_norm(gmix, i)
        if sample:
            for s in range(NS):
                P.dma("sp", scv, V(state_conv.ap[j, s].rearrange("j (c p) -> (j c) p", p=128), []))
                bk = bank()
                tr(bk[:, 0:96], scv, ident_f[0:96, 0:96])
                cp("act", preS[:, :, s, 0:3], bk[:, 0:96].rr("p (j c) -> p c j", j=3))
            mset("pool", qkT, 0.0)
        elif ti == 0:
            mset("pool", carry, 0.0)
            for hg in range(4):
                mset("pool", S[hg], 0.0)
                mset("pool", S_bf[hg], 0.0)

        def v_transpose(c, vsrc):
            bk2 = bank()
            bb = bk2.cast(BF16)
            for sub in range(4):
                tr(bb[:, sub * 128:(sub + 1) * 128], vsrc[:, sub * 128:(sub + 1) * 128], ident_b)
            cp("dve", v_tok[:, :, c - 16, :], bb[:, 0:512].rr("p (s d) -> p s d", s=4))

        for cg in range(12):
            w = wload(sw_gin.ap[j, cg], sw_gin.bufs, KC * 512)
            w3 = w.rr("p (k c) -> p k c", k=KC)
            for q in range(4):
                c = cg * 4 + q
                bk = bank()
                for kc in range(KC):
                    mm(bk, w3[:, kc, q * 128:(q + 1) * 128], hT[:, kc, :], start=(kc == 0), stop=(kc == KC - 1))
                if c >= 32:
                    act(szT[:, c - 32, :], bk, AF.Silu)
                    continue
                if sample:
                    cp("act", preS[:, c, :, 3], bk[:, 0:TT:128])
                    continue
                p_ = pre[c % 2]
                y = cy[c % 2]
                cp("pool", p_[:, 0:3], carry[:, c, :])
                cp("act", p_[:, 3:TT + 3], bk)
                cp("pool", carry[:, c, :], p_[:, TT:TT + 3])
                ts("dve", y, p_[:, 3:TT + 3], convw[:, j, 3, c:c + 1], ALU.mult)
                for tap in range(3):
                    stt("dve", y, p_[:, tap:tap + TT], convw[:, j, tap, c:c + 1], y, ALU.mult, ALU.add)
                if c < 16:
                    act(qkT[:, c, :], y, AF.Silu)
                else:
                    v_ = vT[c % 2]
                    act(v_, y, AF.Silu)
                    v_transpose(c, v_)
        if sample:
            for s in range(NS):
                tt("pool", prodS[:, :, s, :], preS[:, :, s, :], convw[:, j].rr("p j c -> p c j"), ALU.mult)
            red("dve", cyS, prodS)
            act(cySb, cyS, AF.Silu)
            cp("pool", qkT[:, :, 0:TT:128], cySb[:, 0:16, :])
            for c in range(16, 32):
                v_ = vT[c % 2]
                mset("pool", v_, 0.0)
                cp("pool", v_[:, 0:TT:128], cySb[:, c, :])
                v_transpose(c, v_)
            for s in range(NS):
                P.dma("pool", V(s_conv.ap[j, s, 0:2, :], s_conv.bufs), V(state_conv.ap[j, s, 1:3, :], []))
                bk = bank()
                tr(bk[0:32, 0:128], preS[:, :, s, 3], ident_f)
                cp("act", outS[0:32, :], bk[0:32, 0:128])
                P.dma("pool", V(s_conv.ap[j, s, 2, :].rearrange("(c p) -> c p", p=128), s_conv.bufs), outS[0:32, :])
        elif ti == NTP - 1:
            for jj in range(3):
                bk = bank()
                tr(bk[0:32, 0:128], carry[:, :, jj], ident_f)
                cp("act", outS[0:32, :], bk[0:32, 0:128])
                P.dma("pool", V(p_conv.ap[j, jj, :].rearrange("(c p) -> c p", p=128), p_conv.bufs), outS[0:32, :])

        for c in range(16):
            s = sq[c % 2]
            act(s, qkT[:, c, :], AF.Square)
            bk = bank()
            mm(bk, ones_b, s)
            act(lnt, bk, AF.Ln, bias=EPS)
            act(rstd, lnt, AF.Exp, scale=-0.5)
            stt("dve", qkT[:, c, :], qkT[:, c, :], (128.0 ** -0.5) if c < 8 else 1.0, rstd, ALU.mult, ALU.mult)
        for hk in range(8):
            bk2 = bank()
            bb = bk2.cast(BF16)
            for sub in range(4):
                tr(bb[:, sub * 128:(sub + 1) * 128], qkT[:, 8 + hk, sub * 128:(sub + 1) * 128], ident_b)
            cp("act", k_tok[:, :, hk, :], bb[:, 0:512].rr("p (s d) -> p s d", s=4))

        w = wload(sw_gba.ap[j], sw_gba.bufs, KC * 32)
        w3 = w.rr("p (k c) -> p k c", k=KC)
        bk = bank()
        for sub in range(4):
            for kc in range(KC):
                mm(bk[:, sub * 32:(sub + 1) * 32], hT[:, kc, sub * 128:(sub + 1) * 128], w3[:, kc, :],
                   start=(kc == 0), stop=(kc == KC - 1))
        cp("act", ba, bk[:, 0:128].rr("p (s c) -> p s c", s=4))
        act(t16a, ba[:, :, 0:16], AF.Exp, scale=-1.0)
        ts("dve", t16a, t16a, 1.0, ALU.add)
        recip(beta, t16a)
        tt("dve", t16b, ba[:, :, 16:32], dtb[:, j, :].us(1).bc([128, 4, 16]), ALU.add)
        ts("dve", t16a, t16b, -1.0, ALU.mult)
        tt("dve", t16a, t16a, t16b, ALU.max)
        act(t16a, t16a, AF.Exp, scale=-1.0)
        act(t16a, t16a, AF.Ln, bias=1.0)
        stt("dve", gg, t16b, 0.0, t16a, ALU.max, ALU.add)
        tt("dve", gg, gg, nA[:, j, :].us(1).bc([128, 4, 16]), ALU.mult)
        if sample:
            ts("dve", beta, beta, tokmask[:, 0:1], ALU.mult)
            ts("dve", gg, gg, tokmask[:, 0:1], ALU.mult)
        ts("dve", nbeta, beta, -1.0, ALU.mult)

        def b4(v):
            return v.us(2).bc([128, 4, 128])

        def r4(v):
            return v.rr("p (a b) i -> p a b i", a=2)

        def prologue(ch):
            pc = ch % 2
            g_c = gg[:, ch, :]
            bk = bank()
            mm(bk[:, 0:16], Uincl_f, g_c)
            mm(bk[:, 16:32], Lstrict_f, g_c)
            mm(bk[0:16, 32:160], g_c, Uincl_f)
            cp("act", gc2[pc], bk[:, 0:16])
            act(egc2[pc], bk[:, 0:16], AF.Exp)
            act(kds2[pc], bk[:, 16:32], AF.Exp)
            cp("act", gcT2[pc], bk[0:16, 32:160])
            ts("dve", ngc2[pc], gc2[pc], -1.0, ALU.mult)
            bk = bank()
            mm(bk[:, 0:16], lastsel_f, egc2[pc])
            cp("act", glast2[pc], bk[:, 0:16])

        def stageA(ch, hp):
            pc = ch % 2
            cs = slice(ch * 128, (ch + 1) * 128)
            ngc, gcT = ngc2[pc], gcT2[pc]
            for hg in (2 * hp, 2 * hp + 1):
                hs = slice(4 * hg, 4 * hg + 4)
                bkA = bank()
                for q in range(4):
                    mm(bkA[:, q * 128:(q + 1) * 128], SEL[:, 4 * hg + q, :], gcT)
                tt("dve", g_DT[hg], bkA.rr("p (q i) -> p q i", q=4), b4(ngc[:, hs]), ALU.add)
                ts("dve", g_DT[hg], g_DT[hg], 0.0, ALU.min)
                act(g_DT[hg], g_DT[hg], AF.Exp)
                stt("dve", g_DTm[hg], g_DT[hg], 1.0, Uincl4, ALU.min, ALU.mult)
                stt("dve", g_DTs[hg], g_DT[hg], 1.0, Ustrict4, ALU.min, ALU.mult)
                tt("dve", g_DTs[hg], g_DTs[hg], b4(nbeta[:, ch, hs]), ALU.mult)
                yield
                bkB = bank()
                for a_ in range(2):
                    hk = 2 * hg + a_
                    mm(bkB[:, a_ * 128:(a_ + 1) * 128], qkT[:, 8 + hk, cs], qkT[:, 8 + hk, cs])
                    mm(bkB[:, (2 + a_) * 128:(3 + a_) * 128], qkT[:, 8 + hk, cs], qkT[:, hk, cs])
                kk = bkB[:, 0:256].rr("p (a i) -> p a i", a=2).us(2).bc([128, 2, 2, 128])
                qk = bkB[:, 256:512].rr("p (a i) -> p a i", a=2).us(2).bc([128, 2, 2, 128])
                tt("dve", r4(att4[hg]), qk, r4(g_DTm[hg]), ALU.mult)
                if sample:
                    cp("pool", Rfin[hg], ident4_b)
                    yield
                    continue
                tt("dve", r4(g_P[0][hg]), kk, r4(g_DTs[hg]), ALU.mult)
                bkC = bank().cast(BF16)
                for q in range(4):
                    tr(bkC[:, q * 128:(q + 1) * 128], g_P[0][hg][:, q, :], ident_b)
                cp("act", g_PT[0][hg], bkC[:, 0:512].rr("p (q i) -> p q i", q=4))
                yield
            if sample:
                return
            for hg in (2 * hp, 2 * hp + 1):
                Tl, Ru, Xn, XnT = g_PT[1][hg], Rfin[hg], g_P[1][hg], g_R[1][hg]
                tt("dve", Xn, g_PT[0][hg], lvl[:, 0, :].us(1).bc([128, 4, 128]), ALU.mult)
                tt("pool", XnT, g_P[0][hg], lvl[:, 1, :].us(1).bc([128, 4, 128]), ALU.mult)
                tt("dve", Tl, Xn, ident4_b, ALU.add)
                tt("pool", Ru, XnT, ident4_b, ALU.add)
            yield
            for k in range(2, 8):
                for hg in (2 * hp, 2 * hp + 1):
                    Tl, Ru, Xn, XnT = g_PT[1][hg], Rfin[hg], g_P[1][hg], g_R[1][hg]
                    Z1, Z2 = gZ1[hg], gZ2[hg]
                    last = (k == 7)
                    tt("dve", Xn, g_PT[0][hg], lvl[:, 2 * k - 2, :].us(1).bc([128, 4, 128]), ALU.mult)
                    if not last:
                        tt("pool", XnT, g_P[0][hg], lvl[:, 2 * k - 1, :].us(1).bc([128, 4, 128]), ALU.mult)
                        bk = bank()
                        for q in range(4):
                            mm(bk[:, q * 128:(q + 1) * 128], XnT[:, q, :], Tl[:, q, :])
                        cp("act", Z1, bk.rr("p (q i) -> p q i", q=4))
                    bk = bank()
                    for q in range(4):
                        mm(bk[:, q * 128:(q + 1) * 128], Xn[:, q, :], Ru[:, q, :])
                    cp("act" if last else "dve", Z2, bk.rr("p (q i) -> p q i", q=4))
                    yield
                    if not last:
                        bkT_ = bank()
                        for q in range(4):
                            mm(bkT_[:, q * 128:(q + 1) * 128], Ru[:, q, :], Z1[:, q, :])
                    bkR_ = bank()
                    for q in range(4):
                        mm(bkR_[:, q * 128:(q + 1) * 128], Tl[:, q, :], Z2[:, q, :])
                    if not last:
                        tt("dve", Tl, Tl, bkT_.rr("p (q i) -> p q i", q=4), ALU.add)
                    tt("dve", Ru, Ru, bkR_.rr("p (q i) -> p q i", q=4), ALU.add)
                    yield

        def stageB(ch, hp):
            pc = ch % 2
            cs = slice(ch * 128, (ch + 1) * 128)
            egc, kds, glast = egc2[pc], kds2[pc], glast2[pc]
            for hg in (2 * hp, 2 * hp + 1):
                hs = slice(4 * hg, 4 * hg + 4)
                if sample:
                    P.dma("sp", S[hg], V(state_gdn.ap[j, ch, 4 * hg:4 * hg + 4].rearrange("h k v -> k h v"), []))
                    cp("pool", S_bf[hg], S[hg])
                R = Rfin[hg]
                ks = k_tok[:, ch, 2 * hg:2 * hg + 2, :].us(2).bc([128, 2, 2, 128])
                tt("dve", r4(g_kg[hg]), ks, egc[:, hs].rr("p (a b) -> p a b", a=2).us(3).bc([128, 2, 2, 128]), ALU.mult)
                tt("pool", r4(g_kd[hg]), ks, kds[:, hs].rr("p (a b) -> p a b", a=2).us(3).bc([128, 2, 2, 128]), ALU.mult)
                bk = bank()
                for q in range(4):
                    mm(bk[:, q * 128:(q + 1) * 128], g_kg[hg][:, q, :], R[:, q, :])
                act(g_nwT[hg], bk.rr("p (q i) -> p q i", q=4), AF.Copy, scale=-1.0)
                yield
                bk = bank()
                for q in range(4):
                    mm(bk[:, q * 128:(q + 1) * 128], R[:, q, :], v_tok[:, ch, 4 * hg + q, :], start=True, stop=False)
                    mm(bk[:, q * 128:(q + 1) * 128], g_nwT[hg][:, q, :], S_bf[hg][:, q, :], start=False, stop=True)
                tt("dve", g_vn[hg], bk.rr("p (q i) -> p q i", q=4), b4(beta[:, ch, hs]), ALU.mult)
                bk1 = bank()
                for q in range(4):
                    mm(bk1[:, q * 128:(q + 1) * 128], qkT[:, 2 * hg + q // 2, cs], S_bf[hg][:, q, :])
                tt("pool", S[hg], S[hg], b4(glast[:, hs]), ALU.mult)
                yield
                bk2 = bank()
                for q in range(4):
                    mm(bk2[:, q * 128:(q + 1) * 128], att4[hg][:, q, :], g_vn[hg][:, q, :])
                cp("act", g_t[hg], bk2.rr("p (q i) -> p q i", q=4))
                tt("dve", o_tok[hg], bk1.rr("p (q i) -> p q i", q=4), b4(egc[:, hs]), ALU.mult)
                tt("dve", o_tok[hg], o_tok[hg], g_t[hg], ALU.add)
                bk = bank()
                for q in range(4):
                    mm(bk[:, q * 128:(q + 1) * 128], g_kd[hg][:, q, :], g_vn[hg][:, q, :])
                tt("dve", S[hg], S[hg], bk.rr("p (q i) -> p q i", q=4), ALU.add)
                cp("act", S_bf[hg], S[hg])
                yield
                act(osq[hg % 2], o_tok[hg], AF.Square)
                red("dve", oss[:, hs], osq[hg % 2])
                act(olnt[:, hs], oss[:, hs], AF.Ln, bias=EPS, scale=1.0 / 128)
                act(orstd[:, hs], olnt[:, hs], AF.Exp, scale=-0.5)
                tt("dve", on_tok[hg], o_tok[hg], b4(orstd[:, hs]), ALU.mult)
                bkC = bank().cast(BF16)
                for q in range(4):
                    tr(bkC[:, q * 128:(q + 1) * 128], on_tok[hg][:, q, :], ident_b)
                stt("dve", szT[:, hs, cs], bkC[:, 0:512].rr("p (q t) -> p q t", q=4), onorm[:, j:j + 1],
                    szT[:, hs, cs], ALU.mult, ALU.mult)
                if sample:
                    P.dma("pool", V(s_gdn.ap[j, ch, 4 * hg:4 * hg + 4].rearrange("h k v -> k h v"), s_gdn.bufs), S[hg])
                elif ti == NTP - 1 and ch == 3:
                    P.dma("pool", V(p_gdn.ap[j, 4 * hg:4 * hg + 4].rearrange("h k v -> k h v"), p_gdn.bufs), S[hg])
                yield

        units = [(ch, hp) for ch in range(4) for hp in range(2)]
        prologue(0)
        for _ in stageA(*units[0]):
            pass
        for k_ in range(len(units)):
            nxt = units[k_ + 1] if k_ + 1 < len(units) else None
            if nxt is not None and nxt[1] == 0:
                prologue(nxt[0])
            ga = stageA(*nxt) if nxt is not None else iter(())
            gb = stageB(*units[k_])
            da = db = False
            while not (da and db):
                if not da:
                    try:
                        next(ga)
                        next(ga)
                    except StopIteration:
                        da = True
                if not db:
                    try:
                        next(gb)
                    except StopIteration:
                        db = True
        for dc in range(8):
            w = wload(sw_gout.ap[j, dc], sw_gout.bufs, 16 * 128)
            w3 = w.rr("p (k c) -> p k c", k=16)
            bk = bank()
            for kc in range(16):
                mm(bk, w3[:, kc, :], szT[:, kc, :], start=(kc == 0), stop=(kc == 15))
            tt("dve", xT[:, dc, :], xT[:, dc, :], bk, ALU.add)

    AB = [qk_buf, vt_buf]

    def out_kv(g, jb, kvi, half, src, t0, sub, sample):
        win, dil = GROUPS[g]
        hs = slice(half * 4, half * 4 + 4)
        if sample:
            P.dma("pool", V(s_kv[g].ap[jb, sub, win - 1:win, kvi, hs, :], s_kv[g].bufs), src[0:1])
        else:
            w0 = T - min(win, T)
            if t0 >= w0:
                P.dma("pool", V(p_kv[g].ap[jb, t0 - w0:t0 - w0 + 128, kvi, hs, :], p_kv[g].bufs), src)

    def dswA(i, ti):
        jb = i // 2
        sample = ti >= NTP
        pre_norm(gmix, i)
        tok0 = ti * TT
        for cgi in range(18):
            g, typ, half = cgi // 6, (cgi % 6) // 2, cgi % 2
            w = wload(sw_din.ap[jb, cgi], sw_din.bufs, KC * 512)
            w3 = w.rr("p (k c) -> p k c", k=KC)
            if typ < 2:
                gt = gtile[cgi % 2]
                src = (dsw_q_norm if typ == 0 else dsw_k_norm).ap[jb, g:g + 1, :].partition_broadcast(128)
                P.dma("sp", gt, V(src, []))
                tTa = V(arena[:, (cgi % 2) * 2048:(cgi % 2 + 1) * 2048].rearrange("p (h t) -> p h t", h=4), [qk_buf])
            for sub in range(4):
                bk = bank()
                for kc in range(KC):
                    mm(bk, hT[:, kc, sub * 128:(sub + 1) * 128], w3[:, kc, :], start=(kc == 0), stop=(kc == KC - 1))
                bk4 = bk.rr("p (h d) -> p h d", h=4)
                t0 = tok0 + sub * 128
                if typ == 2:
                    vf = g_DT[sub % 2]
                    vb = g_att[sub % 2]
                    cp("act", vf, bk4)
                    cp("dve", vb, bk4)
                    P.dma("pool", V(V_s.ap[g, t0:t0 + 128, half * 4:half * 4 + 4, :], V_s.bufs), vb)
                    out_kv(g, jb, 1, half, vf, t0, sub, sample)
                else:
                    sqh = osq[0]
                    act(sqh, bk4, AF.Square)
                    red("dve", oss[:, 0:4], sqh)
                    act(olnt[:, 0:4], oss[:, 0:4], AF.Ln, bias=EPS, scale=1.0 / 128)
                    act(orstd[:, 0:4], olnt[:, 0:4], AF.Exp, scale=-0.5)
                    nf = o_tok[sub % 2]
                    nb = on_tok[sub % 2]
                    tt("dve", nf, bk4, orstd[:, 0:4].us(2).bc([128, 4, 128]), ALU.mult)
                    tt("dve", nf, nf, gt.us(1).bc([128, 4, 128]), ALU.mult)
                    if typ == 1:
                        out_kv(g, jb, 0, half, nf, t0, sub, sample)
                    cp("act", nb, nf)
                    bkT = bank().cast(BF16)
                    for q in range(4):
                        tr(bkT[:, q * 128:(q + 1) * 128], nb[:, q, :], ident_b)
                    cp("act", tTa[:, :, sub * 128:(sub + 1) * 128], bkT[:, 0:512].rr("p (h t) -> p h t", h=4))
            if typ < 2:
                dst = QT_s if typ == 0 else KT_s
                P.dma("pool", V(dst.ap[g, half * 4:half * 4 + 4, :, tok0:tok0 + TT].rearrange("h p t -> p h t"),
                                dst.bufs), tTa)

    def dswB(i, ti):
        jb = i // 2
        P.dma("sp", szT[:, 0:8, :], V(OT_s.ap[:, :, ti * TT:(ti + 1) * TT].rearrange("h p t -> p h t"), OT_s.bufs))
        for dc in range(8):
            w = wload(sw_dout.ap[jb, dc], sw_dout.bufs, 8 * 128)
            w3 = w.rr("p (k c) -> p k c", k=8)
            bk = bank()
            for kc in range(8):
                mm(bk, w3[:, kc, :], szT[:, kc, :], start=(kc == 0), stop=(kc == 7))
            tt("dve", xT[:, dc, :], xT[:, dc, :], bk, ALU.add)

    def attn(i):
        jb = i // 2
        scale = 128.0 ** -0.5
        bulk_copies(jb)
        acc_o = V(szT.ap.rearrange("p a t -> p (a t)").bitcast(F32), szT.bufs)
        acc_d = V(xT.ap.rearrange("p a t -> p (a t)"), xT.bufs)
        qTg = V(arena[:, 0:TX], AB)
        kTg = V(arena[:, TX:2 * TX], AB)
        Vm = V(arena[:, 2 * TX:2 * TX + T], AB)
        Vs = V(arena[:, 2 * TX + T:3 * TX].rearrange("p (c e) -> p c e", c=4), AB)
        KVc = V(hT.ap.rearrange("p a t -> p (a t)")[:, 0:12 * 256].rearrange("p (x a e) -> p x a e", x=12, a=2), hT.bufs)
        kTc = V(wring[1].ap[:, 0:12 * 128].rearrange("p (x e) -> p x e", x=12), wring[1].bufs)
        obf = wring[0]
        obS = rl[0]
        accS_o = V(cy[0].ap[:, 0:512].rearrange("p (s q) -> p s q", s=4), cy[0].bufs)
        accS_d = V(cy[1].ap[:, 0:512].rearrange("p (s q) -> p s q", s=4), cy[1].bufs)
        Ef = [rstd, lnt]
        Eb = sq
        bctr_ = [0]

        def block(kprev, kcur, q, vprev, vcur, dst_o, dst_d, first):
            n = bctr_[0]
            bctr_[0] += 1
            ef = Ef[n % 2][:, 0:256]
            eb = Eb[n % 2][:, 0:256]
            bkS = bank()
            if kprev is not None:
                mm(bkS[:, 0:128], kprev, q)
            mm(bkS[:, 128:256], kcur, q)
            if kprev is not None:
                act(ef, bkS[:, 0:256], AF.Exp, scale=scale)
                tt("pool", eb.rr("p (a q) -> p a q", a=2), ef.rr("p (a q) -> p a q", a=2), mask2, ALU.mult)
            else:
                act(ef[:, 128:256], bkS[:, 128:256], AF.Exp, scale=scale)
                tt("pool", eb[:, 128:256], ef[:, 128:256], mask2[:, 1, :], ALU.mult)
            def fin():
                bkO = bank()
                if kprev is not None:
                    mm(bkO[:, 0:128], vprev, eb[:, 0:128], start=True, stop=False)
                    mm(bkO[:, 0:128], vcur, eb[:, 128:256], start=False, stop=True)
                    mm(bkO[:, 128:256], ones_b, eb[:, 0:128], start=True, stop=False)
                    mm(bkO[:, 128:256], ones_b, eb[:, 128:256], start=False, stop=True)
                else:
                    mm(bkO[:, 0:128], vcur, eb[:, 128:256])
                    mm(bkO[:, 128:256], ones_b, eb[:, 128:256])
                if first:
                    cp("dve", dst_o, bkO[:, 0:128])
                    cp("dve", dst_d, bkO[:, 128:256])
                else:
                    tt("dve", dst_o, dst_o, bkO[:, 0:128], ALU.add)
                    tt("dve", dst_d, dst_d, bkO[:, 128:256], ALU.add)

            if pend[0] is not None:
                pend[0]()
            pend[0] = fin

        pend = [None]

        def flush():
            if pend[0] is not None:
                pend[0]()
                pend[0] = None

        for h in range(8):
            for s_ in range(NS):
                for g, (win, dil) in enumerate(GROUPS):
                    P.dma("pool", KVc[:, s_ * 3 + g], V(cache[g].ap[jb, s_, 0:win:dil, :, h, :], []))
            for x0 in range(0, 12, 4):
                bkT = bank().cast(BF16)
                for q in range(4):
                    tr(bkT[:, q * 128:(q + 1) * 128], KVc[:, x0 + q, 0, :], ident_b)
                cp("act", kTc[:, x0:x0 + 4, :], bkT[:, 0:512].rr("p (x e) -> p x e", x=4))
            for g, (win, dil) in enumerate(GROUPS):
                P.dma("sp", qTg, V(QT_s.ap[g, h], QT_s.bufs))
                P.dma("sp", kTg, V(KT_s.ap[g, h], KT_s.bufs))
                nbr = T // (128 * dil)
                Vm4 = Vm.rr("p (r n e) -> p r n e", r=dil, n=nbr)
                for r in range(dil):
                    P.dma("sp", Vm4[:, r], V(V_s.ap[g, 0:T, h, :].rearrange("(n i r) e -> r i n e", i=128, r=dil)[r],
                                           V_s.bufs))
                P.dma("sp", Vs, V(V_s.ap[g, T:TX, h, :].rearrange("(c i) e -> i c e", i=128), V_s.bufs))
                for r in range(dil):
                    for n in range(nbr):
                        lo = r + dil * 128 * n
                        qs = slice(lo, lo + dil * 127 + 1, dil)
                        if n > 0:
                            lp = lo - dil * 128
                            ps_ = slice(lp, lp + dil * 127 + 1, dil)
                            block(kTg[:, ps_], kTg[:, qs], qTg[:, qs], Vm4[:, r, n - 1, :], Vm4[:, r, n, :],
                                  acc_o[:, qs], acc_d[:, qs], g == 0)
                        else:
                            block(None, kTg[:, qs], qTg[:, qs], None, Vm4[:, r, n, :], acc_o[:, qs], acc_d[:, qs], g == 0)
                for s_ in range(NS):
                    qs = slice(T + s_ * 128, T + (s_ + 1) * 128)
                    block(kTc[:, s_ * 3 + g, :], kTg[:, qs], qTg[:, qs], KVc[:, s_ * 3 + g, 1, :], Vs[:, s_, :],
                          accS_o[:, s_, :], accS_d[:, s_, :], g == 0)
                flush()
            recip(acc_d[:, 0:T], acc_d[:, 0:T])
            tt("dve", obf[:, 0:T], acc_o[:, 0:T], acc_d[:, 0:T], ALU.mult)
            P.dma("pool", V(OT_s.ap[h, :, 0:T], OT_s.bufs), obf[:, 0:T])
            recip(accS_d, accS_d)
            tt("dve", obS[:, 0:512].rr("p (s q) -> p s q", s=4), accS_o, accS_d, ALU.mult)
            P.dma("pool", V(OT_s.ap[h, :, T:TX], OT_s.bufs), obS[:, 0:512])

    for seg in range(NLB + 1):
        for ti in range(NT):
            load_x_tile(ti, seg == 0)
            if seg >= 1:
                dswB(2 * seg - 1, ti)
                mlp(2 * seg - 1)
            if 2 * seg < DEPTH:
                gdn_layer(2 * seg, ti)
                mlp(2 * seg)
            if 2 * seg + 1 < DEPTH:
                dswA(2 * seg + 1, ti)
            store_x_tile(ti, seg == NLB)
        if 2 * seg + 1 < DEPTH:
            attn(2 * seg + 1)
    P.emit()
    return nc


def _cmask():
    m = np.zeros((14, 128, 128), np.float32)
    i = np.arange(128)[:, None]
    j = np.arange(128)[None, :]
    for k in range(1, 8):
        b = 1 << k
        low = ((i // b) == (j // b)) & ((i % b) >= b // 2) & ((j % b) < b // 2)
        m[2 * (k - 1)] = low
        m[2 * (k - 1) + 1] = low.T
    return m


def run(cfg, in_maps):
    for m in in_maps:
        m["cmask"] = _cmask()
    nc = build(cfg)
    return run_bass_kernel_spmd(nc, in_maps, core_ids=list(range(len(in_maps))))


def kernel(x_prompt, x_sample, state_gdn, state_conv, cache_kv_w128, cache_kv_w512, cache_kv_w2048,
           norm_mix, norm_mlp, gdn_w_in, gdn_conv_w, gdn_a_log, gdn_dt_bias, gdn_o_norm, gdn_w_out,
           dsw_w_in, dsw_q_norm, dsw_k_norm, dsw_w_out, mlp_w_up, mlp_w_down):
    f = lambda a: np.ascontiguousarray(np.asarray(a), dtype=np.float32)
    B, T, _ = x_prompt.shape
    DB = x_sample.shape[0]
    DEPTH = norm_mix.shape[0]
    ncores = 8
    cfg = {"T": T, "DEPTH": DEPTH}
    shared = {"norm_mix": f(norm_mix), "norm_mlp": f(norm_mlp), "gdn_w_in": f(gdn_w_in),
              "gdn_conv_w": f(gdn_conv_w), "gdn_a_log": f(gdn_a_log), "gdn_dt_bias": f(gdn_dt_bias),
              "gdn_o_norm": f(gdn_o_norm), "gdn_w_out": f(gdn_w_out), "mlp_w_up": f(mlp_w_up),
              "mlp_w_down": f(mlp_w_down), "dsw_w_in": f(dsw_w_in), "dsw_q_norm": f(dsw_q_norm),
              "dsw_k_norm": f(dsw_k_norm), "dsw_w_out": f(dsw_w_out)}
    in_maps = []
    for c in range(ncores):
        b = c % B
        sl = slice(c * NS, (c + 1) * NS)
        m = dict(shared)
        m["x_prompt"] = f(x_prompt[b])
        m["x_sample"] = f(x_sample[sl, 0])
        m["state_gdn"] = f(state_gdn[:, sl])
        m["state_conv"] = f(state_conv[:, sl])
        m["cache_kv_w128"] = f(cache_kv_w128[:, sl])
        m["cache_kv_w512"] = f(cache_kv_w512[:, sl])
        m["cache_kv_w2048"] = f(cache_kv_w2048[:, sl])
        in_maps.append(m)
    res = run(cfg, in_maps).results
    NLA = (DEPTH + 1) // 2
    NLB = DEPTH // 2
    y_prompt = np.stack([res[b]["y_prompt"] for b in range(B)]).astype(np.float32)
    y_sample = np.concatenate([res[c]["y_sample"] for c in range(ncores)])[:, None, :].astype(np.float32)
    p_gdn = np.stack([res[b]["p_gdn"] for b in range(B)], axis=1).astype(np.float32)
    p_conv = np.stack([res[b]["p_conv"] for b in range(B)], axis=1).astype(np.float32)
    s_gdn = np.concatenate([res[c]["s_gdn"] for c in range(ncores)], axis=1).astype(np.float32)
    s_conv = np.concatenate([res[c]["s_conv"] for c in range(ncores)], axis=1).astype(np.float32)
    p_kv = [np.stack([res[b]["p_kv%d" % g] for b in range(B)], axis=1).astype(np.float32) for g in range(3)]
    s_kv = [np.concatenate([res[c]["s_kv%d" % g] for c in range(ncores)], axis=1).astype(np.float32) for g in range(3)]
    return (y_prompt, y_sample, p_gdn, p_conv, p_kv[0], p_kv[1], p_kv[2], s_gdn, s_conv, s_kv[0], s_kv[1], s_kv[2])
```
